# Optimizing a Trainium2 kernel written in Bass

```python
import jax
import jax.numpy as jnp
from jax import lax
import numpy as np

D_MODEL = 2048
BATCH = 4
SEQ = 2048
DEPTH = 1

GRID_W = 64
CTX_LEN = 256
HGRN_HEADS = 8
HGRN_HEAD_DIM = 128
HGRN_WIDTH = HGRN_HEADS * HGRN_HEAD_DIM
HGRN_CHUNK = 64
NA_HEADS = 8
NA_HEAD_DIM = 128
NA_WIDTH = NA_HEADS * NA_HEAD_DIM
WIN_R = 8
WIN_C = 16
ROPE_THETA = 10000.0
FFN_HIDDEN = 5632
CONV_W = 3
N_MOD = 6
EPS = 1e-6
IN_COLS = 5 * HGRN_WIDTH + 3 * NA_WIDTH + 2 * D_MODEL

kernel_name = 'hybrid_hgrn2_natten_convffn_dit'


def rmsnorm(x, g):
    xf = x.astype(jnp.float32)
    y = xf * lax.rsqrt(jnp.mean(xf * xf, axis=-1, keepdims=True) + EPS)
    return (y * g.astype(jnp.float32)).astype(x.dtype)


def modulate(h, shift, scale):
    return h * (1.0 + scale) + shift


def split_heads(t, n_heads):
    return t.reshape(t.shape[:-1] + (n_heads, t.shape[-1] // n_heads))


def split_columns(p):
    sizes = (HGRN_WIDTH,) * 5 + (NA_WIDTH,) * 3 + (D_MODEL, D_MODEL)
    outs, off = [], 0
    for s in sizes:
        outs.append(p[..., off:off + s])
        off += s
    return outs


def hgrn2_chunk_scan(q, logf, k, v, s0):
    bsz, L, nh, d = q.shape
    nc = L // HGRN_CHUNK

    def to_chunks(t):
        return t.reshape(bsz, nc, HGRN_CHUNK, nh, d).transpose(1, 0, 3, 2, 4)

    causal = jnp.tril(jnp.ones((HGRN_CHUNK, HGRN_CHUNK), dtype=bool))[:, :, None]

    def step(S, inp):
        qc, gc, kc, vc = inp
        cum = jnp.cumsum(gc, axis=2)
        o_inter = jnp.einsum('bhtk,bhkv->bhtv', qc * jnp.exp(cum), S)
        diff = cum[:, :, :, None, :] - cum[:, :, None, :, :]
        decay = jnp.where(causal, jnp.exp(jnp.where(causal, diff, 0.0)), 0.0)
        scores = jnp.einsum('bhtk,bhsk,bhtsk->bhts', qc, kc, decay)
        o_intra = jnp.einsum('bhts,bhsv->bhtv', scores, vc)
        last = cum[:, :, -1:, :]
        S_new = jnp.exp(last[:, :, 0, :])[..., None] * S + jnp.einsum('bhsk,bhsv->bhkv', kc * jnp.exp(last - cum), vc)
        return S_new, o_inter + o_intra

    s_fin, o = lax.scan(step, s0, (to_chunks(q), to_chunks(logf), to_chunks(k), to_chunks(v)))
    return o.transpose(1, 0, 3, 2, 4).reshape(bsz, L, nh, d), s_fin


def hgrn2_prep(q, f_logit, i_val, lb):
    f = lb + (1.0 - lb) * jax.nn.sigmoid(f_logit.astype(jnp.float32))
    return (split_heads(q.astype(jnp.float32), HGRN_HEADS),
            split_heads(jnp.log(f), HGRN_HEADS),
            split_heads(1.0 - f, HGRN_HEADS),
            split_heads(i_val.astype(jnp.float32), HGRN_HEADS))


def hgrn2_direction(ctx_in, lat_in, lb, reverse):
    ctx_t = hgrn2_prep(ctx_in[0], ctx_in[1], ctx_in[2], lb)
    lat_t = hgrn2_prep(lat_in[0], lat_in[1], lat_in[2], lb)
    if reverse:
        ctx_t = tuple(jnp.flip(t, axis=1) for t in ctx_t)
        lat_t = tuple(jnp.flip(t, axis=1) for t in lat_t)
    bsz = lat_t[0].shape[0]
    s0 = jnp.zeros((bsz, HGRN_HEADS, HGRN_HEAD_DIM, HGRN_HEAD_DIM), jnp.float32)
    o_ctx, s_ctx = hgrn2_chunk_scan(ctx_t[0], ctx_t[1], ctx_t[2], ctx_t[3], s0)
    o_lat, _ = hgrn2_chunk_scan(lat_t[0], lat_t[1], lat_t[2], lat_t[3], s_ctx)
    if reverse:
        o_ctx = jnp.flip(o_ctx, axis=1)
        o_lat = jnp.flip(o_lat, axis=1)
    return o_lat, o_ctx


def hgrn2_readout(o, g, norm_g, dtype):
    on = o * lax.rsqrt(jnp.mean(o * o, axis=-1, keepdims=True) + EPS) * norm_g.astype(jnp.float32)
    y = on.reshape(o.shape[:2] + (HGRN_WIDTH,)) * jax.nn.silu(g.astype(jnp.float32))
    return y.astype(dtype)


def qk_norm(t, g):
    tf = t.astype(jnp.float32)
    return tf * lax.rsqrt(jnp.mean(tf * tf, axis=-1, keepdims=True) + EPS) * g.astype(jnp.float32)


def axial_rope(t):
    L, d = t.shape[1], t.shape[-1]
    pos = jnp.arange(L, dtype=jnp.int32)
    row = (pos // GRID_W).astype(jnp.float32)
    col = (pos % GRID_W).astype(jnp.float32)
    half = d // 2
    nf = half // 2
    inv = ROPE_THETA ** (-jnp.arange(nf, dtype=jnp.float32) / nf)

    def rot(u, p):
        ang = p[:, None] * inv[None, :]
        cos = jnp.cos(ang)[None, :, None, :]
        sin = jnp.sin(ang)[None, :, None, :]
        u1, u2 = u[..., :nf], u[..., nf:]
        return jnp.concatenate([u1 * cos - u2 * sin, u1 * sin + u2 * cos], axis=-1)

    return jnp.concatenate([rot(t[..., :half], row), rot(t[..., half:], col)], axis=-1)


def neighbourhood_attention(q, k, v, k_ctx, v_ctx, rel_bias):
    bsz, L, nh, d = q.shape
    rows = L // GRID_W
    kr = min(WIN_R, rows)
    scale = d ** -0.5
    r = jnp.arange(rows)
    w = jnp.arange(GRID_W)
    row_start = jnp.clip(r - WIN_R // 2, 0, rows - kr)
    row_idx = row_start[:, None] + jnp.arange(kr)[None, :]
    col_start = jnp.clip(w - WIN_C // 2, 0, GRID_W - WIN_C)
    col_in = (w[None, :] >= col_start[:, None]) & (w[None, :] < col_start[:, None] + WIN_C)
    qg = q.reshape(bsz, rows, GRID_W, nh, d)
    kg = k.reshape(bsz, rows, GRID_W, nh, d)[:, row_idx]
    vg = v.reshape(bsz, rows, GRID_W, nh, d)[:, row_idx]
    s_band = jnp.einsum('brchd,brjwhd->bhrcjw', qg, kg).astype(jnp.float32) * scale
    dr = row_idx - r[:, None]
    dc = jnp.clip(w[None, :] - w[:, None], -(WIN_C - 1), WIN_C - 1)
    bias = rel_bias[:, (dr + WIN_R - 1)[:, None, :, None], (dc + WIN_C - 1)[None, :, None, :]]
    s_band = jnp.where(col_in[:, None, :], s_band + bias.astype(jnp.float32)[None], -jnp.inf)
    s_ctx = jnp.einsum('brchd,bnhd->bhrcn', qg, k_ctx).astype(jnp.float32) * scale
    n_band = kr * GRID_W
    s = jnp.concatenate([s_band.reshape(bsz, nh, rows, GRID_W, n_band), s_ctx], axis=-1)
    p = jax.nn.softmax(s, axis=-1)
    p_band = p[..., :n_band].reshape(bsz, nh, rows, GRID_W, kr, GRID_W)
    p_ctx = p[..., n_band:]
    o = jnp.einsum('bhrcjw,brjwhd->brchd', p_band, vg) + jnp.einsum('bhrcn,bnhd->brchd', p_ctx, v_ctx)
    return o.reshape(bsz, L, nh * d)


def context_attention(q, k, v):
    bsz, n, nh, d = q.shape
    s = jnp.einsum('bnhd,bmhd->bhnm', q, k).astype(jnp.float32) * (d ** -0.5)
    p = jax.nn.softmax(s, axis=-1)
    return jnp.einsum('bhnm,bmhd->bnhd', p, v).reshape(bsz, n, nh * d)


def branch_merge(y_a, y_b, gate_a, gate_b, w_a, w_b, w_o):
    z = jax.nn.sigmoid(gate_a) * (y_a @ w_a) + jax.nn.sigmoid(gate_b) * (y_b @ w_b)
    return z @ w_o


def dwconv_centred(u, w, b):
    L = u.shape[1]
    up = jnp.pad(u, ((0, 0), (1, 1), (0, 0)))
    return up[:, 0:L] * w[0] + up[:, 1:L + 1] * w[1] + up[:, 2:L + 2] * w[2] + b


def conv_ffn(h, w1, w3, cw, cb, w2):
    u = dwconv_centred(h @ w1, cw, cb)
    return (jax.nn.silu(u) * (h @ w3)) @ w2


def setup_inputs(seed: int = 0) -> dict:
    key = jax.random.key(seed)
    ks = jax.random.split(key, 24)

    def nrm(k, shape, scale):
        return jax.random.normal(k, shape, jnp.float32) * scale

    return {
        'x': nrm(ks[0], (BATCH, SEQ, D_MODEL), 1.0),
        'c': nrm(ks[1], (BATCH, D_MODEL), 1.0),
        'ctx': nrm(ks[2], (BATCH, CTX_LEN, D_MODEL), 1.0),
        'c_ctx': nrm(ks[3], (D_MODEL,), 1.0),
        'ada_w': nrm(ks[4], (DEPTH, D_MODEL, N_MOD * D_MODEL), D_MODEL ** -0.5),
        'ada_b': nrm(ks[5], (DEPTH, N_MOD * D_MODEL), 0.01),
        'norm1_g': 1.0 + nrm(ks[6], (DEPTH, D_MODEL), 0.02),
        'norm2_g': 1.0 + nrm(ks[7], (DEPTH, D_MODEL), 0.02),
        'w_in': nrm(ks[8], (DEPTH, D_MODEL, IN_COLS), D_MODEL ** -0.5),
        'hgrn_lb_logits': nrm(ks[9], (2, DEPTH + 1, HGRN_WIDTH), 0.5),
        'hgrn_norm_g': 1.0 + nrm(ks[10], (DEPTH, HGRN_HEAD_DIM), 0.02),
        'na_q_norm_g': 1.0 + nrm(ks[11], (DEPTH, NA_HEAD_DIM), 0.02),
        'na_k_norm_g': 1.0 + nrm(ks[12], (DEPTH, NA_HEAD_DIM), 0.02),
        'na_rel_bias': nrm(ks[13], (DEPTH, NA_HEADS, 2 * WIN_R - 1, 2 * WIN_C - 1), 0.1),
        'w_branch_a': nrm(ks[14], (DEPTH, HGRN_WIDTH, D_MODEL), HGRN_WIDTH ** -0.5),
        'w_branch_b': nrm(ks[15], (DEPTH, NA_WIDTH, D_MODEL), NA_WIDTH ** -0.5),
        'w_out': nrm(ks[16], (DEPTH, D_MODEL, D_MODEL), D_MODEL ** -0.5),
        'ffn_w1': nrm(ks[17], (DEPTH, D_MODEL, FFN_HIDDEN), D_MODEL ** -0.5),
        'ffn_w3': nrm(ks[18], (DEPTH, D_MODEL, FFN_HIDDEN), D_MODEL ** -0.5),
        'ffn_conv_w': nrm(ks[19], (DEPTH, CONV_W, FFN_HIDDEN), CONV_W ** -0.5),
        'ffn_conv_b': nrm(ks[20], (DEPTH, FFN_HIDDEN), 0.01),
        'ffn_w2': nrm(ks[21], (DEPTH, FFN_HIDDEN, D_MODEL), FFN_HIDDEN ** -0.5),
    }


def reference(x, c, ctx, c_ctx, ada_w, ada_b, norm1_g, norm2_g, w_in, hgrn_lb_logits, hgrn_norm_g,
              na_q_norm_g, na_k_norm_g, na_rel_bias, w_branch_a, w_branch_b, w_out,
              ffn_w1, ffn_w3, ffn_conv_w, ffn_conv_b, ffn_w2):
    lower_bounds = jnp.cumsum(jax.nn.softmax(hgrn_lb_logits.astype(jnp.float32), axis=1), axis=1)
    xc = ctx
    for l in range(DEPTH):
        last_layer = l == DEPTH - 1
        mod_l = jax.nn.silu(c) @ ada_w[l] + ada_b[l]
        mod_c = jax.nn.silu(c_ctx) @ ada_w[l] + ada_b[l]
        sh1, sc1, g1, sh2, sc2, g2 = [m[:, None, :] for m in jnp.split(mod_l, N_MOD, axis=-1)]
        sh1c, sc1c, g1c, sh2c, sc2c, g2c = jnp.split(mod_c, N_MOD, axis=-1)

        h = modulate(rmsnorm(x, norm1_g[l]), sh1, sc1)
        hc = modulate(rmsnorm(xc, norm1_g[l]), sh1c, sc1c)
        qa, fwa, fba, ia, ga, qn, kn, vn, gta, gtb = split_columns(h @ w_in[l])
        qa_c, fwa_c, fba_c, ia_c, ga_c, qn_c, kn_c, vn_c, gta_c, gtb_c = split_columns(hc @ w_in[l])

        o_lf, o_cf = hgrn2_direction((qa_c, fwa_c, ia_c), (qa, fwa, ia), lower_bounds[0, l], False)
        o_lb, o_cb = hgrn2_direction((qa_c, fba_c, ia_c), (qa, fba, ia), lower_bounds[1, l], True)
        y_a = hgrn2_readout(o_lf + o_lb, ga, hgrn_norm_g[l], x.dtype)

        q_n = axial_rope(qk_norm(split_heads(qn, NA_HEADS), na_q_norm_g[l]))
        k_n = axial_rope(qk_norm(split_heads(kn, NA_HEADS), na_k_norm_g[l]))
        v_n = split_heads(vn, NA_HEADS)
        k_c = qk_norm(split_heads(kn_c, NA_HEADS), na_k_norm_g[l])
        v_c = split_heads(vn_c, NA_HEADS)
        y_b = neighbourhood_attention(q_n, k_n, v_n, k_c, v_c, na_rel_bias[l]).astype(x.dtype)

        x_mid = x + g1 * branch_merge(y_a, y_b, gta, gtb, w_branch_a[l], w_branch_b[l], w_out[l])

        h2 = modulate(rmsnorm(x_mid, norm2_g[l]), sh2, sc2)
        x_new = x_mid + g2 * conv_ffn(h2, ffn_w1[l], ffn_w3[l], ffn_conv_w[l], ffn_conv_b[l], ffn_w2[l])

        if not last_layer:
            y_a_c = hgrn2_readout(o_cf + o_cb, ga_c, hgrn_norm_g[l], x.dtype)
            q_c = qk_norm(split_heads(qn_c, NA_HEADS), na_q_norm_g[l])
            y_b_c = context_attention(q_c, k_c, v_c).astype(x.dtype)
            xc_mid = xc + g1c * branch_merge(y_a_c, y_b_c, gta_c, gtb_c, w_branch_a[l], w_branch_b[l], w_out[l])
            h2c = modulate(rmsnorm(xc_mid, norm2_g[l]), sh2c, sc2c)
            xc = xc_mid + g2c * conv_ffn(h2c, ffn_w1[l], ffn_w3[l], ffn_conv_w[l], ffn_conv_b[l], ffn_w2[l])
        x = x_new
    return x
```

```python
import numpy as np
from contextlib import ExitStack
import concourse.bass as bass
import concourse.mybir as mybir
from concourse.bass_utils import run_bass_kernel_spmd

F32 = mybir.dt.float32
BF16 = mybir.dt.bfloat16
U8 = mybir.dt.uint8
AF = mybir.ActivationFunctionType
ALU = mybir.AluOpType

D = 2048
SEQ = 2048
NCTX = 256
NE = 1152
NOWN = 1024
NT = 2304
HID = 5632
NJ = 44
EPS = 1e-6
NEG = -30000.0
DEBUG = False
STOP = None
KVAR = 0


class Prog:
    ENG = ['pe', 'act', 'dve', 'pool', 'sp']

    def __init__(self, nc, es):
        self.nc = nc
        self.ops = {e: [] for e in self.ENG}
        self.cnt = {e: 0 for e in self.ENG}
        self.sem = {e: es.enter_context(nc.semaphore('s_' + e)) for e in ['pe', 'act', 'dve', 'pool']}
        self.NDS = 12
        self.dsem = [es.enter_context(nc.semaphore('d%d' % i)) for i in range(self.NDS)]
        self.dcnt = [0] * self.NDS
        self.dnext = 0
        self.seen = {e: {} for e in self.ENG}
        self.lastw = {}
        self.readers = {}
        self.bar = []
        self.dead = False

    def barrier(self):
        toks = [(e, self.cnt[e]) for e in ['pe', 'act', 'dve', 'pool'] if self.cnt[e]]
        toks += [(('d', i), self.dcnt[i] * 16) for i in range(self.NDS) if self.dcnt[i]]
        self.bar = toks

    def _deps(self, r, w):
        deps = []
        for k in r:
            if k in self.lastw:
                deps.append(self.lastw[k])
        for k in w:
            if k in self.lastw:
                deps.append(self.lastw[k])
            deps.extend(self.readers.get(k, {}).items())
        return deps

    def _waits(self, eng, deps):
        need = {}
        for sk, v in deps:
            if sk == eng and eng == 'pe':
                continue
            if v > self.seen[eng].get(sk, 0) and v > need.get(sk, 0):
                need[sk] = v
        for sk, v in need.items():
            self.seen[eng][sk] = v
        return list(need.items())

    def _mark(self, token, r, w):
        sk, v = token
        for k in r:
            d = self.readers.setdefault(k, {})
            if d.get(sk, 0) < v:
                d[sk] = v
        for k in w:
            self.lastw[k] = token
            self.readers[k] = {}

    def op(self, eng, fn, r=(), w=()):
        if self.dead:
            return
        waits = self._waits(eng, self._deps(r, w) + self.bar)
        self.cnt[eng] += 1
        self.ops[eng].append((waits, fn, (eng, 1)))
        self._mark((eng, self.cnt[eng]), r, w)

    def dma(self, out, in_, r=(), w=(), q='sp'):
        if self.dead:
            return
        i = self.dnext
        self.dnext = (i + 1) % self.NDS
        deps = self._deps(r, w) + self.bar
        if self.dcnt[i] > 0:
            deps.append((('d', i), self.dcnt[i] * 16))
        waits = self._waits(q, deps)
        self.dcnt[i] += 1
        self.ops[q].append((waits, (lambda e: e.dma_start(out=out, in_=in_)), (('d', i), 16)))
        self._mark((('d', i), self.dcnt[i] * 16), r, w)

    def _s(self, sk):
        return self.sem[sk] if isinstance(sk, str) else self.dsem[sk[1]]

    def emit(self, block):
        def mk(name):
            def f(e):
                for waits, fn, (sk, inc) in self.ops[name]:
                    for wk, v in waits:
                        e.wait_ge(self._s(wk), v)
                    fn(e).then_inc(self._s(sk), inc)
                if name == 'sp':
                    for i in range(self.NDS):
                        if self.dcnt[i]:
                            e.wait_ge(self.dsem[i], self.dcnt[i] * 16)
            return f
        block.tensor(mk('pe'))
        block.scalar(mk('act'))
        block.vector(mk('dve'))
        block.gpsimd(mk('pool'))
        block.sync(mk('sp'))


def build_nc():
    nc = bass.Bass("TRN2", target_bir_lowering=False)

    def din(name, shape, dt=F32):
        return nc.dram_tensor(name, list(shape), dt, kind="ExternalInput").ap()

    xloc = din("xloc", [SEQ, D])
    ctxl = din("ctxl", [NCTX, D])
    cc_d = din("cc", [128, 16, 2])
    adaR = din("adaR", [96, 128, 16, 128])
    adab = din("adab", [128, 96])
    ng_d = din("ng", [128, 2, 16])
    winR = din("winR", [96, 128, 16, 128])
    lbl_d = din("lbl", [128, 2, 2, 8])
    sg_d = din("smallg", [128, 3])
    biasT_d = din("biasT", [128, 8, 20, 64])
    maskT_d = din("maskT", [128, 20, 64])
    rope_d = din("rope", [128, 2, 1408])
    cst_d = din("cst", [128, 384])
    m01_d = din("m01", [128, NT])
    trim_d = din("trim", [64, 2, 64], U8)
    wab_d = din("wab", [2, 16, 128, 8, 128])
    woR = din("woR", [4, 128, 16, 512])
    w1R = din("w1R", [NJ, 128, 16, 128])
    w3R = din("w3R", [NJ, 128, 16, 128])
    w2R = din("w2R", [NJ, 128, D])
    cw_d = din("convw", [128, NJ, 4])
    out_d = nc.dram_tensor("out", [NOWN, D], F32, kind="ExternalOutput").ap()
    sel_d = din("sel", [16, D])
    if DEBUG:
        dbg_y = nc.dram_tensor("dbg_y", [128, 16, NE], BF16, kind="ExternalOutput").ap()
        dbg_xm = nc.dram_tensor("dbg_xm", [128, 9, D], F32, kind="ExternalOutput").ap()
        dbg_h = nc.dram_tensor("dbg_h", [128, 16, NT], BF16, kind="ExternalOutput").ap()

    es = ExitStack()
    with es:
        es.enter_context(nc.allow_low_precision("bf16 matmul operands, fp32 accumulation"))
        es.enter_context(nc.allow_non_contiguous_dma("small layout dmas"))
        P = Prog(nc, es)

        def sb(name, shape, dt=F32):
            return es.enter_context(nc.sbuf_tensor("sb_" + name, list(shape), dt))

        R1 = sb("R1", [128, 18432], F32)
        R2 = sb("R2", [128, 9216], F32)
        WKN = 11776
        WK = sb("WK", [128, WKN], F32)
        hT = R1[:].bitcast(BF16).rearrange("p (k t) -> p k t", k=16)
        xm = R1[:].rearrange("p (t c) -> p t c", t=9)
        r2b = R2[:].bitcast(BF16)
        yab = r2b.rearrange("p (k t) -> p k t", k=16)
        h2T = yab
        wkb = WK[:].bitcast(BF16)
        zT = wkb[:, 0:18432].rearrange("p (k t) -> p k t", k=16)

        NST = 2
        NWB = 4
        stg = [sb("stg%d" % i, [128, 2048], F32) for i in range(NST)]
        wbf = [sb("wbf%d" % i, [128, 2048], BF16) for i in range(NWB)]
        cst = sb("cst", [128, 384], F32)
        cbf = sb("cbf", [128, 384], BF16)
        ident = cbf[:, 0:128]
        ones = cbf[:, 128:256]
        prot = cbf[:, 256:384]
        m01 = sb("m01", [128, 1152], BF16)
        trim = sb("trim", [64, 2, 64], U8)
        cc = sb("cc", [128, 16, 2], F32)
        sil = sb("sil", [128, 16, 2], F32)
        adabs = sb("adabs", [128, 96], F32)
        modT = sb("modT", [128, 96, 2], F32)
        ng = sb("ng", [128, 2, 16], F32)
        a1 = sb("a1", [128, 16, 2], F32)
        a2 = sb("a2", [128, 16], F32)
        gbc = sb("gbc", [128, D], F32)
        grow = sb("grow", [16, 2, 128], F32)
        lbl = sb("lbl", [128, 2, 2, 8], F32)
        lb = sb("lb", [128, 2, 8], F32)
        omlb = sb("omlb", [128, 2, 8], F32)
        nomlb = sb("nomlb", [128, 2, 8], F32)
        smallg = sb("smallg", [128, 3], F32)
        qgs = sb("qgs", [128, 1], F32)
        epsT = sb("epsT", [128, 1], F32)
        ss = sb("ss", [128, 32], F32)
        rs = sb("rs", [128, 32], F32)
        cw = sb("cw", [128, NJ, 4], F32)
        tot = sb("tot", [128, 36], F32)
        etot = sb("etot", [128, 36], F32)
        etoth = sb("etoth", [128, 36], F32)
        toth = sb("toth", [128, 36], F32)
        Sst = sb("Sst", [128, 128], F32)
        Sbf = [sb("Sbf%d" % i, [128, 128], BF16) for i in range(2)]
        ATb = [[sb("AT%d_%d" % (d, i), [64, 64], BF16) for i in range(2)] for d in range(2)]

        ps = [es.enter_context(nc.psum_tensor("ps%d" % i, [128, 512], F32)) for i in range(8)]
        psb = [p[:].bitcast(BF16) for p in ps]
        bank_ctr = [0]

        def nb():
            b = bank_ctr[0] % 6
            bank_ctr[0] += 1
            return b

        lbank = [0]

        def nbl():
            lbank[0] += 1
            return 6 + (lbank[0] % 2)

        def mm(out, lhsT, rhs, start, stop, r, w):
            P.op('pe', lambda e: e.matmul(out, lhsT, rhs, start=start, stop=stop), r=r, w=w)

        def tr(out, in_, r, w):
            P.op('pe', lambda e: e.transpose(out, in_, ident), r=list(r) + ['cbf'], w=w)

        def act(out, in_, func, r, w, bias=None, scale=None, accum=None):
            kw = {}
            if bias is not None:
                kw['bias'] = bias
            if scale is not None:
                kw['scale'] = scale
            if accum is not None:
                kw['accum_out'] = accum
            P.op('act', lambda e: e.activation(out=out, in_=in_, func=func, **kw), r=r, w=w)

        def ts(eng, out, in0, s1, s2, op0, op1, r, w):
            if s2 is None:
                P.op(eng, lambda e: e.tensor_scalar(out=out, in0=in0, scalar1=s1, scalar2=None, op0=op0), r=r, w=w)
            else:
                P.op(eng, lambda e: e.tensor_scalar(out=out, in0=in0, scalar1=s1, scalar2=s2, op0=op0, op1=op1), r=r, w=w)

        def tt(eng, out, in0, in1, op, r, w):
            P.op(eng, lambda e: e.tensor_tensor(out=out, in0=in0, in1=in1, op=op), r=r, w=w)

        def stt(out, in0, scalar, in1, op0, op1, r, w):
            P.op('dve', lambda e: e.scalar_tensor_tensor(out=out, in0=in0, scalar=scalar, in1=in1, op0=op0, op1=op1), r=r, w=w)

        def cp(eng, out, in_, r, w):
            if eng == 'act':
                P.op('act', lambda e: e.activation(out=out, in_=in_, func=AF.Copy), r=r, w=w)
            else:
                P.op(eng, lambda e: e.tensor_copy(out=out, in_=in_), r=r, w=w)

        wctr = [0, 0]
        cast_engs = ['pool', 'act', 'dve', 'pool']

        def wload(src2d, width=2048, mul=None):
            i = wctr[0] % NST
            j = wctr[1] % NWB
            wctr[0] += 1
            wctr[1] += 1
            P.dma(stg[i][:, 0:width], src2d, w=[('stg', i)])
            ce = cast_engs[wctr[0] % 4]
            if mul is not None:
                tt('dve' if ce == 'act' else ce, wbf[j][:, 0:width], stg[i][:, 0:width], mul, ALU.mult,
                   r=[('stg', i), 'gbc'], w=[('wbf', j)])
            else:
                cp(ce, wbf[j][:, 0:width], stg[i][:, 0:width], r=[('stg', i)], w=[('wbf', j)])
            return wbf[j], ('wbf', j)

        P.dma(cst[:], cst_d, w=['cst'])
        cp('dve', cbf[:], cst[:], r=['cst'], w=['cbf'])
        P.dma(WK[:, 0:1152], m01_d[:, 0:1152], w=['wk0'])
        cp('pool', m01[:], WK[:, 0:1152], r=['wk0'], w=['m01'])
        P.dma(trim[:], trim_d, w=['trim'])
        P.dma(cc[:], cc_d, w=['cc'])
        P.dma(adabs[:], adab, w=['adabs'])
        P.dma(ng[:], ng_d, w=['ng'])
        P.dma(lbl[:], lbl_d, w=['lbl'])
        P.dma(smallg[:], sg_d, w=['smallg'])
        P.dma(cw[:], cw_d, w=['cw'])
        P.op('dve', lambda e: e.memset(epsT[:], EPS), w=['epsT'])
        for d in range(2):
            for i in range(2):
                P.op('pool', (lambda d=d, i=i: (lambda e: e.memset(ATb[d][i][:], 0.0)))(), w=[('AT', d, i)])
        tt('dve', lb[:], lbl[:, :, 0, :], lbl[:, :, 1, :], ALU.subtract, r=['lbl'], w=['lb'])
        act(lb[:], lb[:], AF.Sigmoid, r=['lb'], w=['lb'])
        ts('dve', omlb[:], lb[:], -1.0, 1.0, ALU.mult, ALU.add, r=['lb'], w=['omlb'])
        ts('dve', nomlb[:], omlb[:], -1.0, None, ALU.mult, None, r=['omlb'], w=['nomlb'])
        ts('dve', qgs[:], smallg[:, 1:2], float(128 ** -0.5), None, ALU.mult, None, r=['smallg'], w=['qgs'])
        act(sil[:], cc[:], AF.Silu, r=['cc'], w=['sil'])
        pa = 6
        for j in range(96):
            i = j % NST
            P.dma(stg[i][:], adaR[j].rearrange("p k m -> p (k m)"), w=[('stg', i)])
            sv = stg[i][:].rearrange("p (k m) -> p k m", k=16)
            for kc in range(16):
                mm(ps[pa][:, 2 * j:2 * j + 2], sv[:, kc, :], sil[:, kc, :], kc == 0, kc == 15,
                   r=[('stg', i), 'sil'], w=[('ps', pa)])
        tt('dve', modT[:], ps[pa][:, 0:192].rearrange("p (j c) -> p j c", c=2),
           adabs[:].unsqueeze(2).to_broadcast([128, 96, 2]), ALU.add, r=[('ps', pa), 'adabs'], w=['modT'])
        ts('dve', a1[:], modT[:, 16:32, :], 1.0, None, ALU.add, None, r=['modT'], w=['a1'])
        tt('dve', a1[:], a1[:], ng[:, 0, :].unsqueeze(2).to_broadcast([128, 16, 2]), ALU.mult, r=['a1', 'ng'], w=['a1'])
        ts('dve', a2[:], modT[:, 64:80, 0], 1.0, None, ALU.add, None, r=['modT'], w=['a2'])
        tt('dve', a2[:], a2[:], ng[:, 1, :], ALU.mult, r=['a2', 'ng'], w=['a2'])
        for gi, c0_ in enumerate((32, 80)):
            bq = nb()
            P.op('pe', (lambda bq=bq, c0_=c0_: (lambda e: e.transpose(ps[bq][0:16, 0:128], modT[:, c0_:c0_ + 16, 0], cst[:, 0:128])))(),
                 r=['modT', 'cst'], w=[('ps', bq)])
            cp('dve', grow[:, gi, :], ps[bq][0:16, 0:128], r=[('ps', bq)], w=['grow'])

        def load_gbc(gi):
            selT = WK[0:16, 9216:9216 + D]
            P.dma(selT, sel_d, w=['sel'])
            for q4 in range(4):
                bq = nb()
                for kk in range(4):
                    k = q4 * 4 + kk
                    mm(ps[bq][:, kk * 128:(kk + 1) * 128], selT[:, k * 128:(k + 1) * 128], grow[:, gi, :], True, True,
                       r=['sel', 'grow'], w=[('ps', bq)])
                cp('dve', gbc[:, q4 * 512:(q4 + 1) * 512], ps[bq][:, :], r=[('ps', bq)], w=['gbc'])
        if STOP == 'A':
            P.dead = True

        xb = [WK[:, 0:2048], WK[:, 2048:4096]]
        xsb = [wkb[:, 8192:10240], wkb[:, 10240:12288]]
        junkb = [wkb[:, 12288:14336]]

        def norm_tile(idx, src_ap_sb, src_key, dstT, t0, av, bv, evq):
            par = idx % 2
            act(junkb[0], src_ap_sb, AF.Square, r=[src_key], w=['junk', ('ss', idx)], accum=ss[:, idx:idx + 1])
            act(rs[:, idx:idx + 1], ss[:, idx:idx + 1], AF.Ln, r=[('ss', idx), 'epsT'], w=[('rs', idx)],
                bias=epsT[:], scale=1.0 / D)
            act(rs[:, idx:idx + 1], rs[:, idx:idx + 1], AF.Exp, r=[('rs', idx)], w=[('rs', idx)], scale=-0.5)
            ts('dve', xsb[par], src_ap_sb, rs[:, idx:idx + 1], None, ALU.mult, None,
               r=[src_key, ('rs', idx)], w=[('xs', par)])
            for half in range(2):
                b = nb()
                for k8 in range(8):
                    kc = half * 8 + k8
                    tr(psb[b][:, k8 * 128:(k8 + 1) * 128], xsb[par][:, kc * 128:(kc + 1) * 128],
                       r=[('xs', par)], w=[('ps', b)])
                for k8 in range(8):
                    kc = half * 8 + k8
                    o = dstT[:, kc, t0:t0 + 128]
                    i_ = psb[b][:, k8 * 128:(k8 + 1) * 128]
                    if k8 % 2 == 0:
                        act(o, i_, AF.Identity, r=[('ps', b), 'a1', 'a2', 'modT'], w=[evq], bias=bv(kc), scale=av(kc))
                    else:
                        ts('dve', o, i_, av(kc), bv(kc), ALU.mult, ALU.add, r=[('ps', b), 'a1', 'a2', 'modT'], w=[evq])

        P.barrier()
        for ti in range(18):
            par = ti % 2
            src = xloc[ti * 128:(ti + 1) * 128, :] if ti < 16 else ctxl[(ti - 16) * 128:(ti - 15) * 128, :]
            P.dma(xb[par], src, w=[('xb', par)])
            col = 0 if ti < 16 else 1
            norm_tile(ti, xb[par], ('xb', par), hT, ti * 128,
                      (lambda kc, col=col: a1[:, kc, col:col + 1]),
                      (lambda kc, col=col: modT[:, kc, col:col + 1]), 'hT')

        if DEBUG:
            P.dma(dbg_h, hT, r=['hT'], w=['dbg_h'])
        P.barrier()
        if STOP == 'B':
            P.dead = True

        def blocks_of(r0, rn):
            out = []
            t = r0
            while t < r0 + rn:
                n = min(384, r0 + rn - t)
                out.append((t, n))
                t += n
            return out

        def proj_fm(wt, wkey, blocks, consume):
            wv = wt[:].rearrange("p (k m) -> p k m", k=16)
            for (t0, n) in blocks:
                b = nb()
                for kc in range(16):
                    mm(ps[b][:, 0:n], wv[:, kc, :], hT[:, kc, t0:t0 + n], kc == 0, kc == 15,
                       r=[wkey, 'hT'], w=[('ps', b)])
                consume(b, t0, n)

        def wblk(h, g):
            return winR[h * 8 + g].rearrange("p k m -> p (k m)")

        A1 = WK[:, 0:1152]
        A2 = WK[:, 1152:2304]
        A3 = WK[:, 2304:3456]
        oT = WK[:, 3456:4608]
        r32 = WK[:, 4608:4992]
        tmp32 = WK[:, 4992:5376]
        ob = 5376 * 2
        q16 = wkb[:, ob:ob + 1152]
        sga = wkb[:, ob + 1152:ob + 2304]
        Eb1 = wkb[:, ob + 2304:ob + 3456]
        Eb2 = wkb[:, ob + 3456:ob + 4608]
        qe = wkb[:, ob + 4608:ob + 5760]
        ke = wkb[:, ob + 5760:ob + 6912]
        kd = wkb[:, ob + 6912:ob + 8064]
        sqb = wkb[:, ob + 8064:ob + 8448]
        vT = wkb[:, ob + 8448:ob + 8448 + 2304]
        assert ob + 8448 + 2304 <= 2 * WKN
        v64 = r2b[0:64, 9216:9216 + 4608].rearrange("p (c m) -> p c m", c=36)
        kdT = r2b[0:64, 9216 + 4608:9216 + 6912].rearrange("p (c m) -> p c m", c=18)

        def c3(ap):
            return ap.rearrange("p (n t) -> p n t", t=64)

        for h in range(8):
            wt, wk = wload(wblk(h, 0))
            proj_fm(wt, wk, blocks_of(0, 1152), lambda b, t0, n: cp('act', q16[:, t0:t0 + n], ps[b][:, 0:n], r=[('ps', b)], w=['q16']))
            wt, wk = wload(wblk(h, 3))
            proj_fm(wt, wk, blocks_of(0, 1152), lambda b, t0, n: act(sga[:, t0:t0 + n], ps[b][:, 0:n], AF.Silu, r=[('ps', b)], w=['sga']))
            wt, wk = wload(wblk(h, 4))
            proj_fm(wt, wk, blocks_of(0, 2304),
                    lambda b, t0, n: cp('act', vT[:, t0:t0 + n], ps[b][:, 0:n], r=[('ps', b)], w=['vT']))
            for c8 in range(0, 36, 8):
                b = nb()
                m = min(8, 36 - c8)
                for q_ in range(m):
                    ci = c8 + q_
                    tr(psb[b][0:64, q_ * 128:(q_ + 1) * 128], vT[:, ci * 64:(ci + 1) * 64], r=['vT'], w=[('ps', b)])
                cp('dve' if (c8 // 8) % 2 else 'act', v64[:, c8:c8 + m, :],
                   psb[b][0:64, 0:m * 128].rearrange("p (c m) -> p c m", c=m), r=[('ps', b)], w=['v64'])
            for d in range(2):
                wt, wk = wload(wblk(h, 1 + d))
                lbv = lb[:, d, h:h + 1]
                omv = omlb[:, d, h:h + 1]
                nomv = nomlb[:, d, h:h + 1]
                P.op('dve', lambda e: e.memset(Sst[:], 0.0), r=[], w=['S'])
                parts = [((2048, 256), False, [32, 33, 34, 35]), ((0, 1152), True, list(range(18)))] if d == 0 else \
                        [((1152, 1152), False, list(range(35, 17, -1))), ((0, 1152), True, list(range(17, -1, -1)))]
                for (r0, rn), full, order in parts:
                    c0, cn = r0 // 64, rn // 64
                    sl = slice(0, rn)
                    proj_fm(wt, wk, blocks_of(r0, rn),
                            lambda b, t0, n, r0=r0: act(A1[:, t0 - r0:t0 - r0 + n], ps[b][:, 0:n], AF.Sigmoid, r=[('ps', b)], w=['A1']))
                    act(A2[:, sl], A1[:, sl], AF.Ln, r=['A1', 'lb', 'omlb'], w=['A2'], bias=lbv, scale=omv)
                    ts('dve', A1[:, sl], A1[:, sl], nomv, omv, ALU.mult, ALU.add, r=['A1', 'omlb', 'nomlb'], w=['A1'])
                    P.op('dve', (lambda sl=sl: (lambda e: e.tensor_tensor_scan(
                        out=A3[:, sl], data0=m01[:, sl], data1=A2[:, sl], initial=0.0, op0=ALU.mult, op1=ALU.add)))(),
                        r=['A2', 'm01'], w=['A3'])
                    tc_ = tot[:, c0:c0 + cn]
                    cp('dve', tc_, c3(A3[:, sl])[:, :, 63], r=['A3'], w=['tot'])
                    act(etot[:, c0:c0 + cn], tc_, AF.Exp, r=['tot'], w=['etot'])
                    totb = tc_.unsqueeze(2).to_broadcast([128, cn, 64])
                    if d == 0:
                        tt('dve', c3(A2[:, sl]), c3(A3[:, sl]), totb, ALU.subtract, r=['A3', 'tot', 'A2'], w=['A2'])
                        act(Eb1[:, sl], A2[:, sl], AF.Exp, r=['A2'], w=['Eb1'], scale=-1.0)
                    else:
                        tt('dve', A2[:, sl], A3[:, sl], A2[:, sl], ALU.subtract, r=['A3', 'A2'], w=['A2'])
                        act(Eb1[:, sl], A2[:, sl], AF.Exp, r=['A2'], w=['Eb1'])
                    tt('pool', kd[:, sl], A1[:, sl], Eb1[:, sl], ALU.mult, r=['A1', 'Eb1'], w=['kd'])
                    for c8 in range(0, cn, 8):
                        b = nb()
                        m = min(8, cn - c8)
                        for q_ in range(m):
                            cl = c8 + q_
                            tr(psb[b][0:64, q_ * 128:(q_ + 1) * 128], kd[:, cl * 64:(cl + 1) * 64], r=['kd'], w=[('ps', b)])
                        cp('act' if (c8 // 8) % 2 else 'dve', kdT[:, c8:c8 + m, :],
                           psb[b][0:64, 0:m * 128].rearrange("p (c m) -> p c m", c=m), r=[('ps', b)], w=['kdT'])
                    if full:
                        act(etoth[:, 0:18], tot[:, 0:18], AF.Exp, r=['tot'], w=['etoth'], scale=0.5)
                        ts('dve', toth[:, 0:18], tot[:, 0:18], 0.5, None, ALU.mult, None, r=['tot'], w=['toth'])
                        tothb = toth[:, 0:18].unsqueeze(2).to_broadcast([128, 18, 64])
                        if d == 0:
                            tt('dve', c3(A2[:, sl]), c3(A3[:, sl]), tothb, ALU.subtract, r=['A3', 'toth', 'A2'], w=['A2'])
                            Wt, wkey, sq_, sk_ = A2, 'A2', 1.0, -1.0
                        else:
                            tt('dve', c3(A3[:, sl]), c3(A2[:, sl]), tothb, ALU.subtract, r=['A2', 'toth', 'A3'], w=['A3'])
                            Wt, wkey, sq_, sk_ = A3, 'A3', -1.0, 1.0
                        act(Eb1[:, sl], Wt[:, sl], AF.Exp, r=[wkey, 'kd'], w=['Eb1'], scale=sq_)
                        tt('pool', qe, q16, Eb1[:, sl], ALU.mult, r=['q16', 'Eb1'], w=['qe'])
                        act(Eb2[:, sl], Wt[:, sl], AF.Exp, r=[wkey], w=['Eb2'], scale=sk_)
                        tt('dve', ke, A1[:, sl], Eb2[:, sl], ALU.mult, r=['A1', 'Eb2'], w=['ke'])
                    ob_ = None
                    nfull = 0
                    for ci in order:
                        cl = ci - c0
                        if full:
                            sbi = nfull % 2
                            P.op('act', (lambda sbi=sbi, ci=ci: (lambda e: e.activation(
                                out=Sbf[sbi][:], in_=Sst[:], func=AF.Copy, scale=etoth[:, ci:ci + 1])))(),
                                r=['S', 'etoth'], w=[('Sbf', sbi)])
                            b = nb()
                            qs = qe[:, ci * 64:(ci + 1) * 64]
                            mm(ps[b][0:64, 0:64], ke[:, ci * 64:(ci + 1) * 64], qs, True, True, r=['ke', 'qe'], w=[('ps', b)])
                            at = ATb[d][sbi]
                            P.op('dve', (lambda at=at, b=b, d=d: (lambda e: e.copy_predicated(
                                out=at[:], mask=trim[:, d, :], data=ps[b][0:64, 0:64])))(),
                                r=[('ps', b), 'trim'], w=[('AT', d, sbi)])
                            if nfull % 6 == 0:
                                ob_ = nbl()
                            col = (ci % 6) * 64
                            mm(ps[ob_][:, col:col + 64], Sbf[sbi][:], qs, True, False, r=[('Sbf', sbi), 'qe'], w=[('ps', ob_)])
                            mm(ps[ob_][:, col:col + 64], v64[:, ci, :], at[:], False, True, r=['v64', ('AT', d, sbi)], w=[('ps', ob_)])
                            nfull += 1
                            if nfull % 6 == 0:
                                g0 = (ci // 6) * 384
                                if d == 0:
                                    cp('act', oT[:, g0:g0 + 384], ps[ob_][:, 0:384], r=[('ps', ob_)], w=['oT'])
                                else:
                                    tt('dve', oT[:, g0:g0 + 384], oT[:, g0:g0 + 384], ps[ob_][:, 0:384], ALU.add,
                                       r=[('ps', ob_), 'oT'], w=['oT'])
                        b2 = nb()
                        mm(ps[b2][:, 0:128], kdT[:, cl, :], v64[:, ci, :], True, True, r=['kdT', 'v64'], w=[('ps', b2)])
                        stt(Sst[:], Sst[:], etot[:, ci:ci + 1], ps[b2][:, 0:128], ALU.mult, ALU.add,
                            r=['S', 'etot', ('ps', b2)], w=['S'])
            for bi in range(3):
                sl = slice(bi * 384, (bi + 1) * 384)
                act(sqb, oT[:, sl], AF.Square, r=['oT'], w=['sqb'])
                b = nb()
                mm(ps[b][:, 0:384], ones, sqb, True, True, r=['sqb', 'cbf'], w=[('ps', b)])
                act(r32, ps[b][:, 0:384], AF.Ln, r=[('ps', b), 'epsT'], w=['r32'], bias=epsT[:], scale=1.0 / 128)
                act(r32, r32, AF.Exp, r=['r32'], w=['r32'], scale=-0.5)
                stt(tmp32, oT[:, sl], smallg[:, 0:1], r32, ALU.mult, ALU.mult, r=['oT', 'smallg', 'r32'], w=['tmp32'])
                tt('pool', yab[:, h, sl], tmp32, sga[:, sl], ALU.mult, r=['tmp32', 'sga'], w=['yab'])

        P.barrier()
        if STOP == 'C':
            P.dead = True
        ropeT = WK[:, 0:2816].rearrange("p (a t) -> p a t", a=2)
        nr32 = WK[:, 2816:3200]
        nt1 = WK[:, 3200:3584]
        nt2 = WK[:, 3584:3968]
        rcp = WK[:, 3968:4352]
        bstage = WK[:, 4352:5632]
        mstage = WK[:, 5632:6912]
        bo = 6912 * 2
        bT = wkb[:, bo:bo + 1280].rearrange("p (e q) -> p e q", e=20)
        qT = wkb[:, bo + 1280:bo + 2432]
        kT = wkb[:, bo + 2432:bo + 4096]
        ugb = wkb[:, bo + 4096:bo + 4480]
        usq = wkb[:, bo + 4480:bo + 4864]
        vN = wkb[:, bo + 4864:bo + 6528].rearrange("p (t m) -> p t m", t=13)
        pT = [wkb[:, bo + 6528 + i * 448:bo + 6528 + (i + 1) * 448] for i in range(2)]
        vnT = wkb[:, bo + 7424:bo + 7424 + 1664]
        psm = [WK[:, 11456 + i * 64:11456 + (i + 1) * 64] for i in range(2)]
        assert bo + 7424 + 1664 <= 2 * WKN

        P.dma(ropeT, rope_d, w=['rope'])
        P.dma(mstage, maskT_d.rearrange("p e q -> p (e q)"), w=['mstage'])

        def qk_consume(dst, dkey, gcol, rope_on, dst_off, tab_off):
            def f(b, t0, n):
                if KVAR == 1:
                    cp('act', dst[:, dst_off:dst_off + n], ps[b][:, 0:n], r=[('ps', b)], w=[dkey])
                    return
                act(usq[:, 0:n], ps[b][:, 0:n], AF.Square, r=[('ps', b)], w=['usq'])
                act(ugb[:, 0:n], ps[b][:, 0:n], AF.Copy, r=[('ps', b), 'smallg', 'qgs'], w=['ugb'], scale=gcol)
                b1 = nb()
                mm(ps[b1][:, 0:n], ones, usq[:, 0:n], True, True, r=['usq', 'cbf'], w=[('ps', b1)])
                act(nr32[:, 0:n], ps[b1][:, 0:n], AF.Ln, r=[('ps', b1), 'epsT'], w=['nr32'], bias=epsT[:], scale=1.0 / 128)
                act(nr32[:, 0:n], nr32[:, 0:n], AF.Exp, r=['nr32'], w=['nr32'], scale=-0.5)
                o = dst[:, dst_off:dst_off + n]
                if KVAR == 2:
                    tt('dve', o, ugb[:, 0:n], nr32[:, 0:n], ALU.mult, r=['ugb', 'nr32'], w=[dkey])
                    return
                if rope_on:
                    b2 = nb()
                    mm(ps[b2][:, 0:n], prot, ugb[:, 0:n], True, True, r=['ugb', 'cbf'], w=[('ps', b2)])
                    tt('dve', nt1[:, 0:n], ugb[:, 0:n], ropeT[:, 0, tab_off:tab_off + n], ALU.mult, r=['ugb', 'rope'], w=['nt1'])
                    tt('dve', nt2[:, 0:n], ps[b2][:, 0:n], ropeT[:, 1, tab_off:tab_off + n], ALU.mult, r=[('ps', b2), 'rope'], w=['nt2'])
                    tt('dve', nt1[:, 0:n], nt1[:, 0:n], nt2[:, 0:n], ALU.add, r=['nt1', 'nt2'], w=['nt1'])
                    tt('dve', o, nt1[:, 0:n], nr32[:, 0:n], ALU.mult, r=['nt1', 'nr32'], w=[dkey])
                else:
                    tt('dve', o, ugb[:, 0:n], nr32[:, 0:n], ALU.mult, r=['ugb', 'nr32'], w=[dkey])
            return f

        for h in range(1 if KVAR == 3 else 8):
            P.dma(bstage, biasT_d[:, h].rearrange("p e q -> p (e q)"), w=['bstage'])
            tt('dve', bT.rearrange("p e q -> p (e q)"), bstage, mstage, ALU.add, r=['bstage', 'mstage'], w=['bT'])
            if STOP == 'D0':
                P.dead = True
            wt, wk = wload(wblk(h, 5))
            for bi in range(3):
                proj_fm(wt, wk, [(bi * 384, 384)], qk_consume(qT, 'qT', qgs[:, 0:1], True, bi * 384, bi * 384))
            if STOP == 'D1q':
                P.dead = True
            wt, wk = wload(wblk(h, 6))
            for (t0, n) in [(0, 384), (384, 384), (768, 384), (1152, 256)]:
                proj_fm(wt, wk, [(t0, n)], qk_consume(kT, 'kT', smallg[:, 2:3], True, t0, t0))
            proj_fm(wt, wk, [(2048, 256)], qk_consume(kT, 'kT', smallg[:, 2:3], False, 1408, 0))
            if STOP == 'D1k':
                P.dead = True
            wt, wk = wload(wblk(h, 7))
            for (t0, n) in [(0, 384), (384, 384), (768, 384), (1152, 256)]:
                proj_fm(wt, wk, [(t0, n)],
                        lambda b, t0_, n_: cp('act', vnT[:, t0_:t0_ + n_], ps[b][:, 0:n_], r=[('ps', b)], w=['vnT']))
            proj_fm(wt, wk, [(2048, 256)],
                    lambda b, t0_, n_: cp('act', vnT[:, 1408:1664], ps[b][:, 0:n_], r=[('ps', b)], w=['vnT']))
            for g8 in range(0, 13, 8):
                b = nb()
                m = min(8, 13 - g8)
                for q_ in range(m):
                    ti = g8 + q_
                    tr(psb[b][:, q_ * 128:(q_ + 1) * 128], vnT[:, ti * 128:(ti + 1) * 128], r=['vnT'], w=[('ps', b)])
                cp('dve' if (g8 // 8) % 2 else 'act', vN[:, g8:g8 + m, :],
                   psb[b][:, 0:m * 128].rearrange("p (t m) -> p t m", t=m), r=[('ps', b)], w=['vN'])
            if STOP == 'D1':
                P.dead = True
            for g6 in range(3):
                bo_ = 6
                bd_ = 7
                for r6 in range(6):
                    rq = g6 * 6 + r6
                    if rq <= 3:
                        pairs = [0, 2, 4, 6]
                        ents = [10 + (a - rq + 3) for a in pairs]
                    else:
                        a0 = 2 * ((rq - 4) // 2)
                        pairs = [a0 + 2 * i for i in range(5)]
                        ents = [(a - rq + 5) for a in pairs]
                    np_ = len(pairs)
                    ntile = np_ + 2
                    qs = qT[:, rq * 64:(rq + 1) * 64]
                    bs = nb()
                    pi = rq % 2
                    for ti_, a in enumerate(pairs):
                        mm(ps[bs][:, ti_ * 64:(ti_ + 1) * 64], kT[:, a * 64:a * 64 + 128], qs, True, False,
                           r=['kT', 'qT'], w=[('ps', bs)])
                        mm(ps[bs][:, ti_ * 64:(ti_ + 1) * 64], ident, bT[:, ents[ti_], :], False, True,
                           r=['bT', 'cbf'], w=[('ps', bs)])
                    for c_ in range(2):
                        ti_ = np_ + c_
                        mm(ps[bs][:, ti_ * 64:(ti_ + 1) * 64], kT[:, 1408 + c_ * 128:1408 + (c_ + 1) * 128], qs, True, True,
                           r=['kT', 'qT'], w=[('ps', bs)])
                    act(pT[pi][:, 0:ntile * 64], ps[bs][:, 0:ntile * 64], AF.Exp, r=[('ps', bs)], w=[('pT', pi)])
                    vts = [a // 2 for a in pairs] + [11, 12]
                    for ti_ in range(ntile):
                        mm(ps[bo_][:, r6 * 64:(r6 + 1) * 64], vN[:, vts[ti_], :], pT[pi][:, ti_ * 64:(ti_ + 1) * 64],
                           ti_ == 0, ti_ == ntile - 1, r=['vN', ('pT', pi)], w=[('ps', bo_)])
                    P.op('dve', (lambda pi=pi, ntile=ntile: (lambda e: e.tensor_reduce(
                        out=psm[pi], in_=pT[pi][:, 0:ntile * 64].rearrange("p (t q) -> p q t", t=ntile),
                        axis=mybir.AxisListType.X, op=ALU.add)))(), r=[('pT', pi)], w=[('psm', pi)])
                    mm(ps[bd_][:, r6 * 64:(r6 + 1) * 64], cst[:, 128:256], psm[pi], True, True,
                       r=['cst', ('psm', pi)], w=[('ps', bd_)])
                P.op('dve', (lambda bd_=bd_: (lambda e: e.reciprocal(out=rcp, in_=ps[bd_][:, 0:384])))(),
                     r=[('ps', bd_)], w=['rcp'])
                tt('dve', yab[:, 8 + h, g6 * 384:(g6 + 1) * 384], ps[bo_][:, 0:384], rcp, ALU.mult,
                   r=[('ps', bo_), 'rcp'], w=['yab'])
            if STOP == 'D2':
                P.dead = True

        if DEBUG:
            P.dma(dbg_y, yab, r=['yab'], w=['dbg_y'])
        P.barrier()
        if STOP == 'D':
            P.dead = True

        gsa = WK[:, 9216:9600]
        gsb = WK[:, 9600:9984]
        mt1 = WK[:, 9984:10368]
        mt2 = WK[:, 10368:10752]
        for c in range(16):
            wga, kga = wload(winR[64 + 2 * c].rearrange("p k m -> p (k m)"))
            wgb, kgb = wload(winR[64 + 2 * c + 1].rearrange("p k m -> p (k m)"))
            wa, ka = wload(wab_d[0, c].rearrange("p h m -> p (h m)"), width=1024)
            wb_, kb = wload(wab_d[1, c].rearrange("p h m -> p (h m)"), width=1024)
            for bi in range(3):
                sl = slice(bi * 384, (bi + 1) * 384)
                for (wt_, wk_, dst, dk) in [(wga, kga, gsa, 'gsa'), (wgb, kgb, gsb, 'gsb')]:
                    wv = wt_[:].rearrange("p (k m) -> p k m", k=16)
                    b = nb()
                    for kc in range(16):
                        mm(ps[b][:, 0:384], wv[:, kc, :], hT[:, kc, sl], kc == 0, kc == 15, r=[wk_, 'hT'], w=[('ps', b)])
                    act(dst, ps[b][:, 0:384], AF.Sigmoid, r=[('ps', b)], w=[dk])
                for (wt_, wk_, hoff, gs, gk, mt, mk_) in [(wa, ka, 0, gsa, 'gsa', mt1, 'mt1'), (wb_, kb, 8, gsb, 'gsb', mt2, 'mt2')]:
                    wv = wt_[:, 0:1024].rearrange("p (h m) -> p h m", h=8)
                    b = nb()
                    for hh in range(8):
                        mm(ps[b][:, 0:384], wv[:, hh, :], yab[:, hoff + hh, sl], hh == 0, hh == 7, r=[wk_, 'yab'], w=[('ps', b)])
                    tt('dve', mt, ps[b][:, 0:384], gs, ALU.mult, r=[('ps', b), gk], w=[mk_])
                tt('dve', zT[:, c, sl], mt1, mt2, ALU.add, r=['mt1', 'mt2'], w=['zT'])
        P.barrier()
        if STOP == 'E':
            P.dead = True

        load_gbc(0)
        xt = [R2[:, i * 512:(i + 1) * 512] for i in range(2)]
        fm1 = R2[:, 1024:1536]
        for n in range(4):
            wts = []
            for q4 in range(4):
                wts.append(wload(woR[n, :, q4 * 4:(q4 + 1) * 4, :].rearrange("p k c -> p (k c)")))
            for t in range(9):
                par = t % 2
                P.dma(xt[par], xloc[t * 128:(t + 1) * 128, n * 512:(n + 1) * 512], w=[('xt', par)])
                b = nb()
                for kc in range(16):
                    wt_, wk_ = wts[kc // 4]
                    wv = wt_[:].rearrange("p (k c) -> p k c", k=4)
                    mm(ps[b][:, :], zT[:, kc, t * 128:(t + 1) * 128], wv[:, kc % 4, :], kc == 0, kc == 15,
                       r=['zT', wk_], w=[('ps', b)])
                tt('dve', fm1, ps[b][:, :], gbc[:, n * 512:(n + 1) * 512], ALU.mult, r=[('ps', b), 'gbc'], w=['fm1'])
                tt('dve', xm[:, t, n * 512:(n + 1) * 512], fm1, xt[par], ALU.add, r=['fm1', ('xt', par)],
                   w=[('xm', t)])
        if DEBUG:
            P.dma(dbg_xm, xm, r=[('xm', t) for t in range(9)], w=['dbg_xm'])
        P.barrier()
        load_gbc(1)
        xsb[0] = wkb[:, 0:2048]
        xsb[1] = wkb[:, 2048:4096]
        junkb[0] = wkb[:, 4096:6144]
        for t in range(9):
            norm_tile(18 + t, xm[:, t, :], ('xm', t), h2T, t * 128,
                      (lambda kc: a2[:, kc:kc + 1]), (lambda kc: modT[:, 48 + kc, 0:1]), 'yab')
        P.barrier()
        if STOP == 'F':
            P.dead = True

        GS = 5
        u32 = WK[:, 0:1152]
        uc = WK[:, 1152:2176]
        su = WK[:, 2176:3200]
        gated = wkb[:, 6400:6400 + GS * 1024].rearrange("p (j t) -> p j t", j=GS)
        w2b = wkb[:, 6400 + GS * 1024:6400 + GS * 3072].rearrange("p (j c) -> p j c", j=GS)
        assert 6400 + GS * 3072 <= 2 * WKN
        ublk = [(0, 384), (384, 384), (768, 258)]
        j = 0
        while j < NJ:
            gn = min(GS, NJ - j)
            for jj in range(gn):
                jg = j + jj
                w1t, k1 = wload(w1R[jg].rearrange("p k m -> p (k m)"))
                wv = w1t[:].rearrange("p (k m) -> p k m", k=16)
                for (t0, n) in ublk:
                    b = nb()
                    for kc in range(16):
                        mm(ps[b][:, 0:n], wv[:, kc, :], h2T[:, kc, t0:t0 + n], kc == 0, kc == 15, r=[k1, 'yab'], w=[('ps', b)])
                    cp('act', u32[:, t0:t0 + n], ps[b][:, 0:n], r=[('ps', b)], w=['u32'])
                ts('dve', uc, u32[:, 0:1024], cw[:, jg, 1:2], cw[:, jg, 3:4], ALU.mult, ALU.add, r=['u32', 'cw'], w=['uc'])
                stt(uc[:, 1:1024], u32[:, 0:1023], cw[:, jg, 0:1], uc[:, 1:1024], ALU.mult, ALU.add, r=['u32', 'cw', 'uc'], w=['uc'])
                stt(uc, u32[:, 1:1025], cw[:, jg, 2:3], uc, ALU.mult, ALU.add, r=['u32', 'cw', 'uc'], w=['uc'])
                act(su, uc, AF.Silu, r=['uc'], w=['su'])
                w3t, k3 = wload(w3R[jg].rearrange("p k m -> p (k m)"))
                wv3 = w3t[:].rearrange("p (k m) -> p k m", k=16)
                for bi in range(2):
                    b = nb()
                    for kc in range(16):
                        mm(ps[b][:, :], wv3[:, kc, :], h2T[:, kc, bi * 512:(bi + 1) * 512], kc == 0, kc == 15,
                           r=[k3, 'yab'], w=[('ps', b)])
                    tt('dve', gated[:, jj, bi * 512:(bi + 1) * 512], ps[b][:, :], su[:, bi * 512:(bi + 1) * 512], ALU.mult,
                       r=[('ps', b), 'su'], w=['gated'])
                i = wctr[0] % NST
                wctr[0] += 1
                P.dma(stg[i][:], w2R[jg], w=[('stg', i)])
                tt('dve', w2b[:, jj, :], stg[i][:], gbc[:], ALU.mult, r=[('stg', i), 'gbc'], w=['w2b'])
            if STOP == 'G1':
                P.dead = True
            for t in range(8):
                for n in range(4):
                    b = nb()
                    for jj in range(gn):
                        mm(ps[b][:, :], gated[:, jj, t * 128:(t + 1) * 128], w2b[:, jj, n * 512:(n + 1) * 512],
                           jj == 0, jj == gn - 1, r=['gated', 'w2b'], w=[('ps', b)])
                    tt('dve', xm[:, t, n * 512:(n + 1) * 512], xm[:, t, n * 512:(n + 1) * 512], ps[b][:, :], ALU.add,
                       r=[('ps', b), ('xm', t)], w=[('xm', t)])
            if STOP == 'G2':
                P.dead = True
            j += gn
        P.dead = False
        for t in range(8):
            P.dma(out_d[t * 128:(t + 1) * 128, :], xm[:, t, :], r=[('xm', t)], w=[('out', t)])

        blk = es.enter_context(nc.Block())
        P.emit(blk)
    return nc


GRID_W, WIN_R, WIN_C = 64, 8, 16


def _tables(s):
    rows = 32
    kr = 8

    def glob(rl, cl):
        return (rl, cl) if s == 0 else (31 - rl, 63 - cl)

    def entry(rq_l, a_l):
        idr = np.zeros((128, 64), np.int64)
        idc = np.zeros((128, 64), np.int64)
        val = np.zeros((128, 64), bool)
        for kk in range(2):
            krl = a_l + kk
            for kc in range(64):
                for qc in range(64):
                    if krl < 0 or krl > 31:
                        continue
                    qr, qcg = glob(rq_l, qc)
                    kr_g, kcg = glob(krl, kc)
                    rs_ = min(max(qr - WIN_R // 2, 0), rows - kr)
                    cs_ = min(max(qcg - WIN_C // 2, 0), GRID_W - WIN_C)
                    ok = (rs_ <= kr_g < rs_ + kr) and (cs_ <= kcg < cs_ + WIN_C)
                    if ok:
                        dr = kr_g - qr
                        dc = min(max(kcg - qcg, -(WIN_C - 1)), WIN_C - 1)
                        idr[kk * 64 + kc, qc] = dr + WIN_R - 1
                        idc[kk * 64 + kc, qc] = dc + WIN_C - 1
                        val[kk * 64 + kc, qc] = True
        return idr, idc, val

    ents = []
    for e in range(10):
        dra = e - 5
        rq = 8 if dra % 2 == 0 else 9
        ents.append(entry(rq, rq + dra))
    for e in range(10, 20):
        dra = e - 13
        done = False
        for rq in range(4):
            a = rq + dra
            if a in (0, 2, 4, 6):
                ents.append(entry(rq, a))
                done = True
                break
        if not done:
            ents.append((np.zeros((128, 64), np.int64), np.zeros((128, 64), np.int64), np.zeros((128, 64), bool)))
    return ents


_TAB_CACHE = {}


def _consts(s):
    if s in _TAB_CACHE:
        return _TAB_CACHE[s]
    ents = _tables(s)
    idr = np.stack([e[0] for e in ents])
    idc = np.stack([e[1] for e in ents])
    val = np.stack([e[2] for e in ents])
    maskT = np.where(val, 0.0, NEG).astype(np.float32).transpose(1, 0, 2).copy()
    t = np.arange(1408)
    tg = t if s == 0 else (2047 - t)
    row = (tg // GRID_W).astype(np.float32)
    col = (tg % GRID_W).astype(np.float32)
    nf = 32
    inv = (np.float32(10000.0) ** (-np.arange(nf, dtype=np.float32) / np.float32(nf))).astype(np.float32)
    ang_r = (row[:, None] * inv[None, :]).astype(np.float32)
    ang_c = (col[:, None] * inv[None, :]).astype(np.float32)
    cosr, sinr = np.cos(ang_r), np.sin(ang_r)
    cosc, sinc = np.cos(ang_c), np.sin(ang_c)
    cos = np.concatenate([cosr, cosr, cosc, cosc], axis=1).T
    sin = np.concatenate([sinr, sinr, sinc, sinc], axis=1).T
    rope = np.stack([cos, sin], axis=1).astype(np.float32).copy()
    _TAB_CACHE[s] = (idr, idc, maskT, rope)
    return _TAB_CACHE[s]


def _static_consts():
    cst = np.zeros((128, 384), np.float32)
    cst[:, 0:128] = np.eye(128, dtype=np.float32)
    cst[:, 128:256] = 1.0
    pr = np.zeros((128, 128), np.float32)
    for base in (0, 64):
        for j in range(32):
            pr[base + j + 32, base + j] = -1.0
            pr[base + j, base + j + 32] = 1.0
    cst[:, 256:384] = pr
    m01 = np.ones((128, NT), np.float32)
    m01[:, 0::64] = 0.0
    trim = np.zeros((64, 2, 64), np.uint8)
    s_ = np.arange(64)[:, None]
    t_ = np.arange(64)[None, :]
    trim[:, 0, :] = (t_ >= s_)
    trim[:, 1, :] = (t_ <= s_)
    sel = np.zeros((16, D), np.float32)
    for k in range(16):
        sel[k, k * 128:(k + 1) * 128] = 1.0
    return cst, m01, trim, sel


_NC = None


def kernel(x, c, ctx, c_ctx, ada_w, ada_b, norm1_g, norm2_g, w_in, hgrn_lb_logits, hgrn_norm_g,
           na_q_norm_g, na_k_norm_g, na_rel_bias, w_branch_a, w_branch_b, w_out,
           ffn_w1, ffn_w3, ffn_conv_w, ffn_conv_b, ffn_w2, _only_maps=False):
    global _NC
    f = lambda a: np.ascontiguousarray(np.asarray(a, dtype=np.float32))
    x, c, ctx, c_ctx = f(x), f(c), f(ctx), f(c_ctx)
    ada_w, ada_b, w_in = f(ada_w)[0], f(ada_b)[0], f(w_in)[0]
    n1, n2 = f(norm1_g)[0], f(norm2_g)[0]
    lbl = f(hgrn_lb_logits)
    rel = f(na_rel_bias)[0]
    wa, wb, wo = f(w_branch_a)[0], f(w_branch_b)[0], f(w_out)[0]
    w1, w3, w2 = f(ffn_w1)[0], f(ffn_w3)[0], f(ffn_w2)[0]
    cwt, cbs = f(ffn_conv_w)[0], f(ffn_conv_b)[0]

    def colblk(w, ncols):
        return np.ascontiguousarray(w.reshape(16, 128, ncols // 128, 128).transpose(2, 1, 0, 3))

    adaR = colblk(ada_w, 6 * D)
    adab = np.ascontiguousarray(ada_b.reshape(96, 128).T)
    ng = np.ascontiguousarray(np.stack([n1.reshape(16, 128).T, n2.reshape(16, 128).T], axis=1))
    winB = colblk(w_in, 12288)
    smallg = np.ascontiguousarray(np.stack([f(hgrn_norm_g)[0], f(na_q_norm_g)[0], f(na_k_norm_g)[0]], axis=1))
    wab = np.ascontiguousarray(np.stack([wa, wb]).reshape(2, 8, 128, 16, 128).transpose(0, 3, 2, 1, 4))
    woR = np.ascontiguousarray(wo.reshape(16, 128, 4, 512).transpose(2, 1, 0, 3))
    w1R = colblk(w1, HID)
    w3R = colblk(w3, HID)
    w2R = np.ascontiguousarray(w2.reshape(NJ, 128, D))
    cst, m01, trim, sel = _static_consts()
    winR_s = []
    for s in range(2):
        src = [0, 1, 2, 4, 3, 5, 6, 7] if s == 0 else [0, 2, 1, 4, 3, 5, 6, 7]
        arr = np.empty((96, 128, 16, 128), np.float32)
        for h in range(8):
            for g in range(8):
                arr[h * 8 + g] = winB[src[g] * 8 + h]
        for cch in range(16):
            arr[64 + 2 * cch] = winB[64 + cch]
            arr[64 + 2 * cch + 1] = winB[80 + cch]
        winR_s.append(arr)
    in_maps = []
    for core in range(8):
        b, s = core // 2, core % 2
        idr, idc, maskT, rope = _consts(s)
        xl = x[b] if s == 0 else x[b, ::-1]
        cl = ctx[b] if s == 0 else ctx[b, ::-1]
        ccv = np.stack([c[b].reshape(16, 128).T, c_ctx.reshape(16, 128).T], axis=2)
        ld = lbl if s == 0 else lbl[::-1]
        lblc = ld.reshape(2, 2, 8, 128).transpose(3, 0, 1, 2)
        biasT = rel[:, idr, idc].transpose(2, 0, 1, 3)
        taps = cwt if s == 0 else cwt[::-1]
        convw = np.stack([taps[0], taps[1], taps[2], cbs], axis=1).reshape(NJ, 128, 4).transpose(1, 0, 2)
        in_maps.append({
            "xloc": np.ascontiguousarray(xl), "ctxl": np.ascontiguousarray(cl),
            "cc": np.ascontiguousarray(ccv), "adaR": adaR, "adab": adab, "ng": ng,
            "winR": winR_s[s], "lbl": np.ascontiguousarray(lblc), "smallg": smallg,
            "biasT": np.ascontiguousarray(biasT), "maskT": maskT, "rope": rope,
            "cst": cst, "m01": m01, "trim": trim, "sel": sel, "wab": wab, "woR": woR,
            "w1R": w1R, "w3R": w3R, "w2R": w2R, "convw": np.ascontiguousarray(convw),
        })
    if _only_maps:
        return in_maps
    if _NC is None:
        _NC = build_nc()
    res = run_bass_kernel_spmd(_NC, in_maps, core_ids=list(range(8)))
    out = np.empty((4, SEQ, D), np.float32)
    for core in range(8):
        b, s = core // 2, core % 2
        o = np.asarray(res.results[core]["out"], dtype=np.float32)
        if s == 0:
            out[b, 0:1024] = o
        else:
            out[b, 1024:2048] = o[::-1]
    kernel.last_results = res
    return out
```

```python
import numpy as np
from contextlib import ExitStack
import concourse.bass as bass
import concourse.mybir as mybir
from concourse.bass_utils import run_bass_kernel_spmd

F32 = mybir.dt.float32
BF16 = mybir.dt.bfloat16
U8 = mybir.dt.uint8
AF = mybir.ActivationFunctionType
ALU = mybir.AluOpType

D = 2048
SEQ = 2048
NCTX = 256
NE = 1152
NOWN = 1024
NT = 2304
HID = 5632
NJ = 44
EPS = 1e-6
NEG = -30000.0
DEBUG = False
STOP = None
KVAR = 0


class Prog:
    ENG = ['pe', 'act', 'dve', 'pool', 'sp']

    def __init__(self, nc, es):
        self.nc = nc
        self.ops = {e: [] for e in self.ENG}
        self.cnt = {e: 0 for e in self.ENG}
        self.sem = {e: es.enter_context(nc.semaphore('s_' + e)) for e in ['pe', 'act', 'dve', 'pool']}
        self.NDS = 12
        self.dsem = [es.enter_context(nc.semaphore('d%d' % i)) for i in range(self.NDS)]
        self.dcnt = [0] * self.NDS
        self.dnext = 0
        self.seen = {e: {} for e in self.ENG}
        self.lastw = {}
        self.readers = {}
        self.bar = []
        self.dead = False

    def barrier(self):
        toks = [(e, self.cnt[e]) for e in ['pe', 'act', 'dve', 'pool'] if self.cnt[e]]
        toks += [(('d', i), self.dcnt[i] * 16) for i in range(self.NDS) if self.dcnt[i]]
        self.bar = toks

    def _deps(self, r, w):
        deps = []
        for k in r:
            if k in self.lastw:
                deps.append(self.lastw[k])
        for k in w:
            if k in self.lastw:
                deps.append(self.lastw[k])
            deps.extend(self.readers.get(k, {}).items())
        return deps

    def _waits(self, eng, deps):
        need = {}
        for sk, v in deps:
            if sk == eng and eng == 'pe':
                continue
            if v > self.seen[eng].get(sk, 0) and v > need.get(sk, 0):
                need[sk] = v
        for sk, v in need.items():
            self.seen[eng][sk] = v
        return list(need.items())

    def _mark(self, token, r, w):
        sk, v = token
        for k in r:
            d = self.readers.setdefault(k, {})
            if d.get(sk, 0) < v:
                d[sk] = v
        for k in w:
            self.lastw[k] = token
            self.readers[k] = {}

    def op(self, eng, fn, r=(), w=()):
        if self.dead:
            return
        waits = self._waits(eng, self._deps(r, w) + self.bar)
        self.cnt[eng] += 1
        self.ops[eng].append((waits, fn, (eng, 1)))
        self._mark((eng, self.cnt[eng]), r, w)

    def dma(self, out, in_, r=(), w=(), q='sp'):
        if self.dead:
            return
        i = self.dnext
        self.dnext = (i + 1) % self.NDS
        deps = self._deps(r, w) + self.bar
        if self.dcnt[i] > 0:
            deps.append((('d', i), self.dcnt[i] * 16))
        waits = self._waits(q, deps)
        self.dcnt[i] += 1
        self.ops[q].append((waits, (lambda e: e.dma_start(out=out, in_=in_)), (('d', i), 16)))
        self._mark((('d', i), self.dcnt[i] * 16), r, w)

    def _s(self, sk):
        return self.sem[sk] if isinstance(sk, str) else self.dsem[sk[1]]

    def emit(self, block):
        def mk(name):
            def f(e):
                for waits, fn, (sk, inc) in self.ops[name]:
                    for wk, v in waits:
                        e.wait_ge(self._s(wk), v)
                    fn(e).then_inc(self._s(sk), inc)
                if name == 'sp':
                    for i in range(self.NDS):
                        if self.dcnt[i]:
                            e.wait_ge(self.dsem[i], self.dcnt[i] * 16)
            return f
        block.tensor(mk('pe'))
        block.scalar(mk('act'))
        block.vector(mk('dve'))
        block.gpsimd(mk('pool'))
        block.sync(mk('sp'))


def build_nc():
    nc = bass.Bass("TRN2", target_bir_lowering=False)

    def din(name, shape, dt=F32):
        return nc.dram_tensor(name, list(shape), dt, kind="ExternalInput").ap()

    xloc = din("xloc", [SEQ, D])
    ctxl = din("ctxl", [NCTX, D])
    cc_d = din("cc", [128, 16, 2])
    adaR = din("adaR", [96, 128, 16, 128])
    adab = din("adab", [128, 96])
    ng_d = din("ng", [128, 2, 16])
    winR = din("winR", [96, 128, 16, 128])
    lbl_d = din("lbl", [128, 2, 2, 8])
    sg_d = din("smallg", [128, 3])
    biasT_d = din("biasT", [128, 8, 20, 64])
    maskT_d = din("maskT", [128, 20, 64])
    rope_d = din("rope", [128, 2, 1408])
    cst_d = din("cst", [128, 384])
    m01_d = din("m01", [128, NT])
    trim_d = din("trim", [64, 2, 64], U8)
    wab_d = din("wab", [2, 16, 128, 8, 128])
    woR = din("woR", [4, 128, 16, 512])
    w1R = din("w1R", [NJ, 128, 16, 128])
    w3R = din("w3R", [NJ, 128, 16, 128])
    w2R = din("w2R", [NJ, 128, D])
    cw_d = din("convw", [128, NJ, 4])
    out_d = nc.dram_tensor("out", [NOWN, D], F32, kind="ExternalOutput").ap()
    sel_d = din("sel", [16, D])
    if DEBUG:
        dbg_y = nc.dram_tensor("dbg_y", [128, 16, NE], BF16, kind="ExternalOutput").ap()
        dbg_xm = nc.dram_tensor("dbg_xm", [128, 9, D], F32, kind="ExternalOutput").ap()
        dbg_h = nc.dram_tensor("dbg_h", [128, 16, NT], BF16, kind="ExternalOutput").ap()

    es = ExitStack()
    with es:
        es.enter_context(nc.allow_low_precision("bf16 matmul operands, fp32 accumulation"))
        es.enter_context(nc.allow_non_contiguous_dma("small layout dmas"))
        P = Prog(nc, es)

        def sb(name, shape, dt=F32):
            return es.enter_context(nc.sbuf_tensor("sb_" + name, list(shape), dt))

        R1 = sb("R1", [128, 18432], F32)
        R2 = sb("R2", [128, 9216], F32)
        WKN = 11776
        WK = sb("WK", [128, WKN], F32)
        hT = R1[:].bitcast(BF16).rearrange("p (k t) -> p k t", k=16)
        xm = R1[:].rearrange("p (t c) -> p t c", t=9)
        r2b = R2[:].bitcast(BF16)
        yab = r2b.rearrange("p (k t) -> p k t", k=16)
        h2T = yab
        wkb = WK[:].bitcast(BF16)
        zT = wkb[:, 0:18432].rearrange("p (k t) -> p k t", k=16)

        NST = 2
        NWB = 4
        stg = [sb("stg%d" % i, [128, 2048], F32) for i in range(NST)]
        wbf = [sb("wbf%d" % i, [128, 2048], BF16) for i in range(NWB)]
        cst = sb("cst", [128, 384], F32)
        cbf = sb("cbf", [128, 384], BF16)
        ident = cbf[:, 0:128]
        ones = cbf[:, 128:256]
        prot = cbf[:, 256:384]
        m01 = sb("m01", [128, 1152], BF16)
        trim = sb("trim", [64, 2, 64], U8)
        cc = sb("cc", [128, 16, 2], F32)
        sil = sb("sil", [128, 16, 2], F32)
        adabs = sb("adabs", [128, 96], F32)
        modT = sb("modT", [128, 96, 2], F32)
        ng = sb("ng", [128, 2, 16], F32)
        a1 = sb("a1", [128, 16, 2], F32)
        a2 = sb("a2", [128, 16], F32)
        gbc = sb("gbc", [128, D], F32)
        grow = sb("grow", [16, 2, 128], F32)
        lbl = sb("lbl", [128, 2, 2, 8], F32)
        lb = sb("lb", [128, 2, 8], F32)
        omlb = sb("omlb", [128, 2, 8], F32)
        nomlb = sb("nomlb", [128, 2, 8], F32)
        smallg = sb("smallg", [128, 3], F32)
        qgs = sb("qgs", [128, 1], F32)
        epsT = sb("epsT", [128, 1], F32)
        ss = sb("ss", [128, 32], F32)
        rs = sb("rs", [128, 32], F32)
        cw = sb("cw", [128, NJ, 4], F32)
        tot = sb("tot", [128, 36], F32)
        etot = sb("etot", [128, 36], F32)
        etoth = sb("etoth", [128, 36], F32)
        toth = sb("toth", [128, 36], F32)
        Sst = sb("Sst", [128, 128], F32)
        Sbf = [sb("Sbf%d" % i, [128, 128], BF16) for i in range(2)]
        ATb = [[sb("AT%d_%d" % (d, i), [64, 64], BF16) for i in range(2)] for d in range(2)]

        ps = [es.enter_context(nc.psum_tensor("ps%d" % i, [128, 512], F32)) for i in range(8)]
        psb = [p[:].bitcast(BF16) for p in ps]
        bank_ctr = [0]

        def nb():
            b = bank_ctr[0] % 6
            bank_ctr[0] += 1
            return b

        lbank = [0]

        def nbl():
            lbank[0] += 1
            return 6 + (lbank[0] % 2)

        def mm(out, lhsT, rhs, start, stop, r, w):
            P.op('pe', lambda e: e.matmul(out, lhsT, rhs, start=start, stop=stop), r=r, w=w)

        def tr(out, in_, r, w):
            P.op('pe', lambda e: e.transpose(out, in_, ident), r=list(r) + ['cbf'], w=w)

        def act(out, in_, func, r, w, bias=None, scale=None, accum=None):
            kw = {}
            if bias is not None:
                kw['bias'] = bias
            if scale is not None:
                kw['scale'] = scale
            if accum is not None:
                kw['accum_out'] = accum
            P.op('act', lambda e: e.activation(out=out, in_=in_, func=func, **kw), r=r, w=w)

        def ts(eng, out, in0, s1, s2, op0, op1, r, w):
            if s2 is None:
                P.op(eng, lambda e: e.tensor_scalar(out=out, in0=in0, scalar1=s1, scalar2=None, op0=op0), r=r, w=w)
            else:
                P.op(eng, lambda e: e.tensor_scalar(out=out, in0=in0, scalar1=s1, scalar2=s2, op0=op0, op1=op1), r=r, w=w)

        def tt(eng, out, in0, in1, op, r, w):
            P.op(eng, lambda e: e.tensor_tensor(out=out, in0=in0, in1=in1, op=op), r=r, w=w)

        def stt(out, in0, scalar, in1, op0, op1, r, w):
            P.op('dve', lambda e: e.scalar_tensor_tensor(out=out, in0=in0, scalar=scalar, in1=in1, op0=op0, op1=op1), r=r, w=w)

        def cp(eng, out, in_, r, w):
            if eng == 'act':
                P.op('act', lambda e: e.activation(out=out, in_=in_, func=AF.Copy), r=r, w=w)
            else:
                P.op(eng, lambda e: e.tensor_copy(out=out, in_=in_), r=r, w=w)

        wctr = [0, 0]
        cast_engs = ['pool', 'act', 'dve', 'pool']

        def wload(src2d, width=2048, mul=None):
            i = wctr[0] % NST
            j = wctr[1] % NWB
            wctr[0] += 1
            wctr[1] += 1
            P.dma(stg[i][:, 0:width], src2d, w=[('stg', i)])
            ce = cast_engs[wctr[0] % 4]
            if mul is not None:
                tt('dve' if ce == 'act' else ce, wbf[j][:, 0:width], stg[i][:, 0:width], mul, ALU.mult,
                   r=[('stg', i), 'gbc'], w=[('wbf', j)])
            else:
                cp(ce, wbf[j][:, 0:width], stg[i][:, 0:width], r=[('stg', i)], w=[('wbf', j)])
            return wbf[j], ('wbf', j)

        P.dma(cst[:], cst_d, w=['cst'])
        cp('dve', cbf[:], cst[:], r=['cst'], w=['cbf'])
        P.dma(WK[:, 0:1152], m01_d[:, 0:1152], w=['wk0'])
        cp('pool', m01[:], WK[:, 0:1152], r=['wk0'], w=['m01'])
        P.dma(trim[:], trim_d, w=['trim'])
        P.dma(cc[:], cc_d, w=['cc'])
        P.dma(adabs[:], adab, w=['adabs'])
        P.dma(ng[:], ng_d, w=['ng'])
        P.dma(lbl[:], lbl_d, w=['lbl'])
        P.dma(smallg[:], sg_d, w=['smallg'])
        P.dma(cw[:], cw_d, w=['cw'])
        P.op('dve', lambda e: e.memset(epsT[:], EPS), w=['epsT'])
        for d in range(2):
            for i in range(2):
                P.op('pool', (lambda d=d, i=i: (lambda e: e.memset(ATb[d][i][:], 0.0)))(), w=[('AT', d, i)])
        tt('dve', lb[:], lbl[:, :, 0, :], lbl[:, :, 1, :], ALU.subtract, r=['lbl'], w=['lb'])
        act(lb[:], lb[:], AF.Sigmoid, r=['lb'], w=['lb'])
        ts('dve', omlb[:], lb[:], -1.0, 1.0, ALU.mult, ALU.add, r=['lb'], w=['omlb'])
        ts('dve', nomlb[:], omlb[:], -1.0, None, ALU.mult, None, r=['omlb'], w=['nomlb'])
        ts('dve', qgs[:], smallg[:, 1:2], float(128 ** -0.5), None, ALU.mult, None, r=['smallg'], w=['qgs'])
        act(sil[:], cc[:], AF.Silu, r=['cc'], w=['sil'])
        pa = 6
        for j in range(96):
            i = j % NST
            P.dma(stg[i][:], adaR[j].rearrange("p k m -> p (k m)"), w=[('stg', i)])
            sv = stg[i][:].rearrange("p (k m) -> p k m", k=16)
            for kc in range(16):
                mm(ps[pa][:, 2 * j:2 * j + 2], sv[:, kc, :], sil[:, kc, :], kc == 0, kc == 15,
                   r=[('stg', i), 'sil'], w=[('ps', pa)])
        tt('dve', modT[:], ps[pa][:, 0:192].rearrange("p (j c) -> p j c", c=2),
           adabs[:].unsqueeze(2).to_broadcast([128, 96, 2]), ALU.add, r=[('ps', pa), 'adabs'], w=['modT'])
        ts('dve', a1[:], modT[:, 16:32, :], 1.0, None, ALU.add, None, r=['modT'], w=['a1'])
        tt('dve', a1[:], a1[:], ng[:, 0, :].unsqueeze(2).to_broadcast([128, 16, 2]), ALU.mult, r=['a1', 'ng'], w=['a1'])
        ts('dve', a2[:], modT[:, 64:80, 0], 1.0, None, ALU.add, None, r=['modT'], w=['a2'])
        tt('dve', a2[:], a2[:], ng[:, 1, :], ALU.mult, r=['a2', 'ng'], w=['a2'])
        for gi, c0_ in enumerate((32, 80)):
            bq = nb()
            P.op('pe', (lambda bq=bq, c0_=c0_: (lambda e: e.transpose(ps[bq][0:16, 0:128], modT[:, c0_:c0_ + 16, 0], cst[:, 0:128])))(),
                 r=['modT', 'cst'], w=[('ps', bq)])
            cp('dve', grow[:, gi, :], ps[bq][0:16, 0:128], r=[('ps', bq)], w=['grow'])

        def load_gbc(gi):
            selT = WK[0:16, 9216:9216 + D]
            P.dma(selT, sel_d, w=['sel'])
            for q4 in range(4):
                bq = nb()
                for kk in range(4):
                    k = q4 * 4 + kk
                    mm(ps[bq][:, kk * 128:(kk + 1) * 128], selT[:, k * 128:(k + 1) * 128], grow[:, gi, :], True, True,
                       r=['sel', 'grow'], w=[('ps', bq)])
                cp('dve', gbc[:, q4 * 512:(q4 + 1) * 512], ps[bq][:, :], r=[('ps', bq)], w=['gbc'])
        if STOP == 'A':
            P.dead = True

        xb = [WK[:, 0:2048], WK[:, 2048:4096]]
        xsb = [wkb[:, 8192:10240], wkb[:, 10240:12288]]
        junkb = [wkb[:, 12288:14336]]

        def norm_tile(idx, src_ap_sb, src_key, dstT, t0, av, bv, evq):
            par = idx % 2
            act(junkb[0], src_ap_sb, AF.Square, r=[src_key], w=['junk', ('ss', idx)], accum=ss[:, idx:idx + 1])
            act(rs[:, idx:idx + 1], ss[:, idx:idx + 1], AF.Ln, r=[('ss', idx), 'epsT'], w=[('rs', idx)],
                bias=epsT[:], scale=1.0 / D)
            act(rs[:, idx:idx + 1], rs[:, idx:idx + 1], AF.Exp, r=[('rs', idx)], w=[('rs', idx)], scale=-0.5)
            ts('dve', xsb[par], src_ap_sb, rs[:, idx:idx + 1], None, ALU.mult, None,
               r=[src_key, ('rs', idx)], w=[('xs', par)])
            for half in range(2):
                b = nb()
                for k8 in range(8):
                    kc = half * 8 + k8
                    tr(psb[b][:, k8 * 128:(k8 + 1) * 128], xsb[par][:, kc * 128:(kc + 1) * 128],
                       r=[('xs', par)], w=[('ps', b)])
                for k8 in range(8):
                    kc = half * 8 + k8
                    o = dstT[:, kc, t0:t0 + 128]
                    i_ = psb[b][:, k8 * 128:(k8 + 1) * 128]
                    if half == 0:
                        act(o, i_, AF.Identity, r=[('ps', b), 'a1', 'a2', 'modT'], w=[evq], bias=bv(kc), scale=av(kc))
                    else:
                        ts('dve', o, i_, av(kc), bv(kc), ALU.mult, ALU.add, r=[('ps', b), 'a1', 'a2', 'modT'], w=[evq])

        P.barrier()
        for ti in range(18):
            par = ti % 2
            src = xloc[ti * 128:(ti + 1) * 128, :] if ti < 16 else ctxl[(ti - 16) * 128:(ti - 15) * 128, :]
            P.dma(xb[par], src, w=[('xb', par)])
            col = 0 if ti < 16 else 1
            norm_tile(ti, xb[par], ('xb', par), hT, ti * 128,
                      (lambda kc, col=col: a1[:, kc, col:col + 1]),
                      (lambda kc, col=col: modT[:, kc, col:col + 1]), 'hT')

        if DEBUG:
            P.dma(dbg_h, hT, r=['hT'], w=['dbg_h'])
        P.barrier()
        if STOP == 'B':
            P.dead = True

        def blocks_of(r0, rn):
            out = []
            t = r0
            while t < r0 + rn:
                n = min(384, r0 + rn - t)
                out.append((t, n))
                t += n
            return out

        def proj_fm(wt, wkey, blocks, consume):
            wv = wt[:].rearrange("p (k m) -> p k m", k=16)
            for (t0, n) in blocks:
                b = nb()
                for kc in range(16):
                    mm(ps[b][:, 0:n], wv[:, kc, :], hT[:, kc, t0:t0 + n], kc == 0, kc == 15,
                       r=[wkey, 'hT'], w=[('ps', b)])
                consume(b, t0, n)

        def wblk(h, g):
            return winR[h * 8 + g].rearrange("p k m -> p (k m)")

        A1 = WK[:, 0:1152]
        A2 = WK[:, 1152:2304]
        A3 = WK[:, 2304:3456]
        oT = WK[:, 3456:4608]
        r32 = WK[:, 4608:4992]
        tmp32 = WK[:, 4992:5376]
        ob = 5376 * 2
        q16 = wkb[:, ob:ob + 1152]
        sga = wkb[:, ob + 1152:ob + 2304]
        Eb1 = wkb[:, ob + 2304:ob + 3456]
        Eb2 = wkb[:, ob + 3456:ob + 4608]
        qe = wkb[:, ob + 4608:ob + 5760]
        ke = wkb[:, ob + 5760:ob + 6912]
        kd = wkb[:, ob + 6912:ob + 8064]
        sqb = wkb[:, ob + 8064:ob + 8448]
        vT = wkb[:, ob + 8448:ob + 8448 + 2304]
        assert ob + 8448 + 2304 <= 2 * WKN
        v64 = r2b[0:64, 9216:9216 + 4608].rearrange("p (c m) -> p c m", c=36)
        kdT = r2b[0:64, 9216 + 4608:9216 + 6912].rearrange("p (c m) -> p c m", c=18)

        def c3(ap):
            return ap.rearrange("p (n t) -> p n t", t=64)

        for h in range(8):
            wt, wk = wload(wblk(h, 0))
            proj_fm(wt, wk, blocks_of(0, 1152), lambda b, t0, n: cp('act', q16[:, t0:t0 + n], ps[b][:, 0:n], r=[('ps', b)], w=['q16']))
            wt, wk = wload(wblk(h, 3))
            proj_fm(wt, wk, blocks_of(0, 1152), lambda b, t0, n: act(sga[:, t0:t0 + n], ps[b][:, 0:n], AF.Silu, r=[('ps', b)], w=['sga']))
            wt, wk = wload(wblk(h, 4))
            proj_fm(wt, wk, blocks_of(0, 2304),
                    lambda b, t0, n: cp('act', vT[:, t0:t0 + n], ps[b][:, 0:n], r=[('ps', b)], w=['vT']))
            for c8 in range(0, 36, 8):
                b = nb()
                m = min(8, 36 - c8)
                for q_ in range(m):
                    ci = c8 + q_
                    tr(psb[b][0:64, q_ * 128:(q_ + 1) * 128], vT[:, ci * 64:(ci + 1) * 64], r=['vT'], w=[('ps', b)])
                cp('dve' if (c8 // 8) % 2 else 'act', v64[:, c8:c8 + m, :],
                   psb[b][0:64, 0:m * 128].rearrange("p (c m) -> p c m", c=m), r=[('ps', b)], w=['v64'])
            for d in range(2):
                wt, wk = wload(wblk(h, 1 + d))
                lbv = lb[:, d, h:h + 1]
                omv = omlb[:, d, h:h + 1]
                nomv = nomlb[:, d, h:h + 1]
                P.op('dve', lambda e: e.memset(Sst[:], 0.0), r=[], w=['S'])
                parts = [((2048, 256), False, [32, 33, 34, 35]), ((0, 1152), True, list(range(18)))] if d == 0 else \
                        [((1152, 1152), False, list(range(35, 17, -1))), ((0, 1152), True, list(range(17, -1, -1)))]
                for (r0, rn), full, order in parts:
                    c0, cn = r0 // 64, rn // 64
                    sl = slice(0, rn)
                    proj_fm(wt, wk, blocks_of(r0, rn),
                            lambda b, t0, n, r0=r0: act(A1[:, t0 - r0:t0 - r0 + n], ps[b][:, 0:n], AF.Sigmoid, r=[('ps', b)], w=['A1']))
                    act(A2[:, sl], A1[:, sl], AF.Ln, r=['A1', 'lb', 'omlb'], w=['A2'], bias=lbv, scale=omv)
                    ts('dve', A1[:, sl], A1[:, sl], nomv, omv, ALU.mult, ALU.add, r=['A1', 'omlb', 'nomlb'], w=['A1'])
                    P.op('dve', (lambda sl=sl: (lambda e: e.tensor_tensor_scan(
                        out=A3[:, sl], data0=m01[:, sl], data1=A2[:, sl], initial=0.0, op0=ALU.mult, op1=ALU.add)))(),
                        r=['A2', 'm01'], w=['A3'])
                    tc_ = tot[:, c0:c0 + cn]
                    cp('dve', tc_, c3(A3[:, sl])[:, :, 63], r=['A3'], w=['tot'])
                    act(etot[:, c0:c0 + cn], tc_, AF.Exp, r=['tot'], w=['etot'])
                    totb = tc_.unsqueeze(2).to_broadcast([128, cn, 64])
                    if d == 0:
                        tt('dve', c3(A2[:, sl]), c3(A3[:, sl]), totb, ALU.subtract, r=['A3', 'tot', 'A2'], w=['A2'])
                        act(Eb1[:, sl], A2[:, sl], AF.Exp, r=['A2'], w=['Eb1'], scale=-1.0)
                    else:
                        tt('dve', A2[:, sl], A3[:, sl], A2[:, sl], ALU.subtract, r=['A3', 'A2'], w=['A2'])
                        act(Eb1[:, sl], A2[:, sl], AF.Exp, r=['A2'], w=['Eb1'])
                    tt('pool', kd[:, sl], A1[:, sl], Eb1[:, sl], ALU.mult, r=['A1', 'Eb1'], w=['kd'])
                    for c8 in range(0, cn, 8):
                        b = nb()
                        m = min(8, cn - c8)
                        for q_ in range(m):
                            cl = c8 + q_
                            tr(psb[b][0:64, q_ * 128:(q_ + 1) * 128], kd[:, cl * 64:(cl + 1) * 64], r=['kd'], w=[('ps', b)])
                        cp('act' if (c8 // 8) % 2 else 'dve', kdT[:, c8:c8 + m, :],
                           psb[b][0:64, 0:m * 128].rearrange("p (c m) -> p c m", c=m), r=[('ps', b)], w=['kdT'])
                    if full:
                        act(etoth[:, 0:18], tot[:, 0:18], AF.Exp, r=['tot'], w=['etoth'], scale=0.5)
                        ts('dve', toth[:, 0:18], tot[:, 0:18], 0.5, None, ALU.mult, None, r=['tot'], w=['toth'])
                        tothb = toth[:, 0:18].unsqueeze(2).to_broadcast([128, 18, 64])
                        if d == 0:
                            tt('dve', c3(A2[:, sl]), c3(A3[:, sl]), tothb, ALU.subtract, r=['A3', 'toth', 'A2'], w=['A2'])
                            Wt, wkey, sq_, sk_ = A2, 'A2', 1.0, -1.0
                        else:
                            tt('dve', c3(A3[:, sl]), c3(A2[:, sl]), tothb, ALU.subtract, r=['A2', 'toth', 'A3'], w=['A3'])
                            Wt, wkey, sq_, sk_ = A3, 'A3', -1.0, 1.0
                        act(Eb1[:, sl], Wt[:, sl], AF.Exp, r=[wkey, 'kd'], w=['Eb1'], scale=sq_)
                        tt('pool', qe, q16, Eb1[:, sl], ALU.mult, r=['q16', 'Eb1'], w=['qe'])
                        act(Eb2[:, sl], Wt[:, sl], AF.Exp, r=[wkey], w=['Eb2'], scale=sk_)
                        tt('dve', ke, A1[:, sl], Eb2[:, sl], ALU.mult, r=['A1', 'Eb2'], w=['ke'])
                    ob_ = None
                    nfull = 0
                    for ci in order:
                        cl = ci - c0
                        if full:
                            sbi = nfull % 2
                            P.op('act', (lambda sbi=sbi, ci=ci: (lambda e: e.activation(
                                out=Sbf[sbi][:], in_=Sst[:], func=AF.Copy, scale=etoth[:, ci:ci + 1])))(),
                                r=['S', 'etoth'], w=[('Sbf', sbi)])
                            b = nb()
                            qs = qe[:, ci * 64:(ci + 1) * 64]
                            mm(ps[b][0:64, 0:64], ke[:, ci * 64:(ci + 1) * 64], qs, True, True, r=['ke', 'qe'], w=[('ps', b)])
                            at = ATb[d][sbi]
                            P.op('dve', (lambda at=at, b=b, d=d: (lambda e: e.copy_predicated(
                                out=at[:], mask=trim[:, d, :], data=ps[b][0:64, 0:64])))(),
                                r=[('ps', b), 'trim'], w=[('AT', d, sbi)])
                            if nfull % 6 == 0:
                                ob_ = nbl()
                            col = (ci % 6) * 64
                            mm(ps[ob_][:, col:col + 64], Sbf[sbi][:], qs, True, False, r=[('Sbf', sbi), 'qe'], w=[('ps', ob_)])
                            mm(ps[ob_][:, col:col + 64], v64[:, ci, :], at[:], False, True, r=['v64', ('AT', d, sbi)], w=[('ps', ob_)])
                            nfull += 1
                            if nfull % 6 == 0:
                                g0 = (ci // 6) * 384
                                if d == 0:
                                    cp('act', oT[:, g0:g0 + 384], ps[ob_][:, 0:384], r=[('ps', ob_)], w=['oT'])
                                else:
                                    tt('dve', oT[:, g0:g0 + 384], oT[:, g0:g0 + 384], ps[ob_][:, 0:384], ALU.add,
                                       r=[('ps', ob_), 'oT'], w=['oT'])
                        b2 = nb()
                        mm(ps[b2][:, 0:128], kdT[:, cl, :], v64[:, ci, :], True, True, r=['kdT', 'v64'], w=[('ps', b2)])
                        stt(Sst[:], Sst[:], etot[:, ci:ci + 1], ps[b2][:, 0:128], ALU.mult, ALU.add,
                            r=['S', 'etot', ('ps', b2)], w=['S'])
            for bi in range(3):
                sl = slice(bi * 384, (bi + 1) * 384)
                act(sqb, oT[:, sl], AF.Square, r=['oT'], w=['sqb'])
                b = nb()
                mm(ps[b][:, 0:384], ones, sqb, True, True, r=['sqb', 'cbf'], w=[('ps', b)])
                act(r32, ps[b][:, 0:384], AF.Ln, r=[('ps', b), 'epsT'], w=['r32'], bias=epsT[:], scale=1.0 / 128)
                act(r32, r32, AF.Exp, r=['r32'], w=['r32'], scale=-0.5)
                stt(tmp32, oT[:, sl], smallg[:, 0:1], r32, ALU.mult, ALU.mult, r=['oT', 'smallg', 'r32'], w=['tmp32'])
                tt('pool', yab[:, h, sl], tmp32, sga[:, sl], ALU.mult, r=['tmp32', 'sga'], w=['yab'])

        P.barrier()
        if STOP == 'C':
            P.dead = True
        ropeT = WK[:, 0:2816].rearrange("p (a t) -> p a t", a=2)
        nr32 = WK[:, 2816:3200]
        nt1 = WK[:, 3200:3584]
        nt2 = WK[:, 3584:3968]
        rcp = WK[:, 3968:4352]
        bstage = WK[:, 4352:5632]
        mstage = WK[:, 5632:6912]
        bo = 6912 * 2
        bT = wkb[:, bo:bo + 1280].rearrange("p (e q) -> p e q", e=20)
        qT = wkb[:, bo + 1280:bo + 2432]
        kT = wkb[:, bo + 2432:bo + 4096]
        ugb = wkb[:, bo + 4096:bo + 4480]
        usq = wkb[:, bo + 4480:bo + 4864]
        vN = wkb[:, bo + 4864:bo + 6528].rearrange("p (t m) -> p t m", t=13)
        pT = [wkb[:, bo + 6528 + i * 448:bo + 6528 + (i + 1) * 448] for i in range(2)]
        vnT = wkb[:, bo + 7424:bo + 7424 + 1664]
        psm = [WK[:, 11456 + i * 64:11456 + (i + 1) * 64] for i in range(2)]
        assert bo + 7424 + 1664 <= 2 * WKN

        P.dma(ropeT, rope_d, w=['rope'])
        P.dma(mstage, maskT_d.rearrange("p e q -> p (e q)"), w=['mstage'])

        def qk_consume(dst, dkey, gcol, rope_on, dst_off, tab_off):
            def f(b, t0, n):
                if KVAR == 1:
                    cp('act', dst[:, dst_off:dst_off + n], ps[b][:, 0:n], r=[('ps', b)], w=[dkey])
                    return
                act(usq[:, 0:n], ps[b][:, 0:n], AF.Square, r=[('ps', b)], w=['usq'])
                act(ugb[:, 0:n], ps[b][:, 0:n], AF.Copy, r=[('ps', b), 'smallg', 'qgs'], w=['ugb'], scale=gcol)
                b1 = nb()
                mm(ps[b1][:, 0:n], ones, usq[:, 0:n], True, True, r=['usq', 'cbf'], w=[('ps', b1)])
                act(nr32[:, 0:n], ps[b1][:, 0:n], AF.Ln, r=[('ps', b1), 'epsT'], w=['nr32'], bias=epsT[:], scale=1.0 / 128)
                act(nr32[:, 0:n], nr32[:, 0:n], AF.Exp, r=['nr32'], w=['nr32'], scale=-0.5)
                o = dst[:, dst_off:dst_off + n]
                if KVAR == 2:
                    tt('dve', o, ugb[:, 0:n], nr32[:, 0:n], ALU.mult, r=['ugb', 'nr32'], w=[dkey])
                    return
                if rope_on:
                    b2 = nb()
                    mm(ps[b2][:, 0:n], prot, ugb[:, 0:n], True, True, r=['ugb', 'cbf'], w=[('ps', b2)])
                    tt('dve', nt1[:, 0:n], ugb[:, 0:n], ropeT[:, 0, tab_off:tab_off + n], ALU.mult, r=['ugb', 'rope'], w=['nt1'])
                    tt('dve', nt2[:, 0:n], ps[b2][:, 0:n], ropeT[:, 1, tab_off:tab_off + n], ALU.mult, r=[('ps', b2), 'rope'], w=['nt2'])
                    tt('dve', nt1[:, 0:n], nt1[:, 0:n], nt2[:, 0:n], ALU.add, r=['nt1', 'nt2'], w=['nt1'])
                    tt('dve', o, nt1[:, 0:n], nr32[:, 0:n], ALU.mult, r=['nt1', 'nr32'], w=[dkey])
                else:
                    tt('dve', o, ugb[:, 0:n], nr32[:, 0:n], ALU.mult, r=['ugb', 'nr32'], w=[dkey])
            return f

        for h in range(1 if KVAR == 3 else 8):
            P.dma(bstage, biasT_d[:, h].rearrange("p e q -> p (e q)"), w=['bstage'])
            tt('dve', bT.rearrange("p e q -> p (e q)"), bstage, mstage, ALU.add, r=['bstage', 'mstage'], w=['bT'])
            if STOP == 'D0':
                P.dead = True
            wt, wk = wload(wblk(h, 5))
            for bi in range(3):
                proj_fm(wt, wk, [(bi * 384, 384)], qk_consume(qT, 'qT', qgs[:, 0:1], True, bi * 384, bi * 384))
            if STOP == 'D1q':
                P.dead = True
            wt, wk = wload(wblk(h, 6))
            for (t0, n) in [(0, 384), (384, 384), (768, 384), (1152, 256)]:
                proj_fm(wt, wk, [(t0, n)], qk_consume(kT, 'kT', smallg[:, 2:3], True, t0, t0))
            proj_fm(wt, wk, [(2048, 256)], qk_consume(kT, 'kT', smallg[:, 2:3], False, 1408, 0))
            if STOP == 'D1k':
                P.dead = True
            wt, wk = wload(wblk(h, 7))
            for (t0, n) in [(0, 384), (384, 384), (768, 384), (1152, 256)]:
                proj_fm(wt, wk, [(t0, n)],
                        lambda b, t0_, n_: cp('act', vnT[:, t0_:t0_ + n_], ps[b][:, 0:n_], r=[('ps', b)], w=['vnT']))
            proj_fm(wt, wk, [(2048, 256)],
                    lambda b, t0_, n_: cp('act', vnT[:, 1408:1664], ps[b][:, 0:n_], r=[('ps', b)], w=['vnT']))
            for g8 in range(0, 13, 8):
                b = nb()
                m = min(8, 13 - g8)
                for q_ in range(m):
                    ti = g8 + q_
                    tr(psb[b][:, q_ * 128:(q_ + 1) * 128], vnT[:, ti * 128:(ti + 1) * 128], r=['vnT'], w=[('ps', b)])
                cp('dve' if (g8 // 8) % 2 else 'act', vN[:, g8:g8 + m, :],
                   psb[b][:, 0:m * 128].rearrange("p (t m) -> p t m", t=m), r=[('ps', b)], w=['vN'])
            if STOP == 'D1':
                P.dead = True
            for g6 in range(3):
                bo_ = 6
                bd_ = 7
                for r6 in range(6):
                    rq = g6 * 6 + r6
                    if rq <= 3:
                        pairs = [0, 2, 4, 6]
                        ents = [10 + (a - rq + 3) for a in pairs]
                    else:
                        a0 = 2 * ((rq - 4) // 2)
                        pairs = [a0 + 2 * i for i in range(5)]
                        ents = [(a - rq + 5) for a in pairs]
                    np_ = len(pairs)
                    ntile = np_ + 2
                    qs = qT[:, rq * 64:(rq + 1) * 64]
                    bs = nb()
                    pi = rq % 2
                    for ti_, a in enumerate(pairs):
                        mm(ps[bs][:, ti_ * 64:(ti_ + 1) * 64], kT[:, a * 64:a * 64 + 128], qs, True, False,
                           r=['kT', 'qT'], w=[('ps', bs)])
                        mm(ps[bs][:, ti_ * 64:(ti_ + 1) * 64], ident, bT[:, ents[ti_], :], False, True,
                           r=['bT', 'cbf'], w=[('ps', bs)])
                    for c_ in range(2):
                        ti_ = np_ + c_
                        mm(ps[bs][:, ti_ * 64:(ti_ + 1) * 64], kT[:, 1408 + c_ * 128:1408 + (c_ + 1) * 128], qs, True, True,
                           r=['kT', 'qT'], w=[('ps', bs)])
                    act(pT[pi][:, 0:ntile * 64], ps[bs][:, 0:ntile * 64], AF.Exp, r=[('ps', bs)], w=[('pT', pi)])
                    vts = [a // 2 for a in pairs] + [11, 12]
                    for ti_ in range(ntile):
                        mm(ps[bo_][:, r6 * 64:(r6 + 1) * 64], vN[:, vts[ti_], :], pT[pi][:, ti_ * 64:(ti_ + 1) * 64],
                           ti_ == 0, ti_ == ntile - 1, r=['vN', ('pT', pi)], w=[('ps', bo_)])
                    P.op('dve', (lambda pi=pi, ntile=ntile: (lambda e: e.tensor_reduce(
                        out=psm[pi], in_=pT[pi][:, 0:ntile * 64].rearrange("p (t q) -> p q t", t=ntile),
                        axis=mybir.AxisListType.X, op=ALU.add)))(), r=[('pT', pi)], w=[('psm', pi)])
                    mm(ps[bd_][:, r6 * 64:(r6 + 1) * 64], cst[:, 128:256], psm[pi], True, True,
                       r=['cst', ('psm', pi)], w=[('ps', bd_)])
                P.op('dve', (lambda bd_=bd_: (lambda e: e.reciprocal(out=rcp, in_=ps[bd_][:, 0:384])))(),
                     r=[('ps', bd_)], w=['rcp'])
                tt('dve', yab[:, 8 + h, g6 * 384:(g6 + 1) * 384], ps[bo_][:, 0:384], rcp, ALU.mult,
                   r=[('ps', bo_), 'rcp'], w=['yab'])
            if STOP == 'D2':
                P.dead = True

        if DEBUG:
            P.dma(dbg_y, yab, r=['yab'], w=['dbg_y'])
        P.barrier()
        if STOP == 'D':
            P.dead = True

        gsa = WK[:, 9216:9600]
        gsb = WK[:, 9600:9984]
        mt1 = WK[:, 9984:10368]
        mt2 = WK[:, 10368:10752]
        for c in range(16):
            wga, kga = wload(winR[64 + 2 * c].rearrange("p k m -> p (k m)"))
            wgb, kgb = wload(winR[64 + 2 * c + 1].rearrange("p k m -> p (k m)"))
            wa, ka = wload(wab_d[0, c].rearrange("p h m -> p (h m)"), width=1024)
            wb_, kb = wload(wab_d[1, c].rearrange("p h m -> p (h m)"), width=1024)
            for bi in range(3):
                sl = slice(bi * 384, (bi + 1) * 384)
                for (wt_, wk_, dst, dk) in [(wga, kga, gsa, 'gsa'), (wgb, kgb, gsb, 'gsb')]:
                    wv = wt_[:].rearrange("p (k m) -> p k m", k=16)
                    b = nb()
                    for kc in range(16):
                        mm(ps[b][:, 0:384], wv[:, kc, :], hT[:, kc, sl], kc == 0, kc == 15, r=[wk_, 'hT'], w=[('ps', b)])
                    act(dst, ps[b][:, 0:384], AF.Sigmoid, r=[('ps', b)], w=[dk])
                for (wt_, wk_, hoff, gs, gk, mt, mk_) in [(wa, ka, 0, gsa, 'gsa', mt1, 'mt1'), (wb_, kb, 8, gsb, 'gsb', mt2, 'mt2')]:
                    wv = wt_[:, 0:1024].rearrange("p (h m) -> p h m", h=8)
                    b = nb()
                    for hh in range(8):
                        mm(ps[b][:, 0:384], wv[:, hh, :], yab[:, hoff + hh, sl], hh == 0, hh == 7, r=[wk_, 'yab'], w=[('ps', b)])
                    tt('dve', mt, ps[b][:, 0:384], gs, ALU.mult, r=[('ps', b), gk], w=[mk_])
                tt('dve', zT[:, c, sl], mt1, mt2, ALU.add, r=['mt1', 'mt2'], w=['zT'])
        P.barrier()
        if STOP == 'E':
            P.dead = True

        load_gbc(0)
        xt = [R2[:, i * 512:(i + 1) * 512] for i in range(2)]
        fm1 = R2[:, 1024:1536]
        for n in range(4):
            wts = []
            for q4 in range(4):
                wts.append(wload(woR[n, :, q4 * 4:(q4 + 1) * 4, :].rearrange("p k c -> p (k c)")))
            for t in range(9):
                par = t % 2
                P.dma(xt[par], xloc[t * 128:(t + 1) * 128, n * 512:(n + 1) * 512], w=[('xt', par)])
                b = nb()
                for kc in range(16):
                    wt_, wk_ = wts[kc // 4]
                    wv = wt_[:].rearrange("p (k c) -> p k c", k=4)
                    mm(ps[b][:, :], zT[:, kc, t * 128:(t + 1) * 128], wv[:, kc % 4, :], kc == 0, kc == 15,
                       r=['zT', wk_], w=[('ps', b)])
                tt('dve', fm1, ps[b][:, :], gbc[:, n * 512:(n + 1) * 512], ALU.mult, r=[('ps', b), 'gbc'], w=['fm1'])
                tt('dve', xm[:, t, n * 512:(n + 1) * 512], fm1, xt[par], ALU.add, r=['fm1', ('xt', par)],
                   w=[('xm', t)])
        if DEBUG:
            P.dma(dbg_xm, xm, r=[('xm', t) for t in range(9)], w=['dbg_xm'])
        P.barrier()
        load_gbc(1)
        xsb[0] = wkb[:, 0:2048]
        xsb[1] = wkb[:, 2048:4096]
        junkb[0] = wkb[:, 4096:6144]
        for t in range(9):
            norm_tile(18 + t, xm[:, t, :], ('xm', t), h2T, t * 128,
                      (lambda kc: a2[:, kc:kc + 1]), (lambda kc: modT[:, 48 + kc, 0:1]), 'yab')
        P.barrier()
        if STOP == 'F':
            P.dead = True

        GS = 5
        u32 = WK[:, 0:1152]
        uc = WK[:, 1152:2176]
        su = WK[:, 2176:3200]
        gated = wkb[:, 6400:6400 + GS * 1024].rearrange("p (j t) -> p j t", j=GS)
        w2b = wkb[:, 6400 + GS * 1024:6400 + GS * 3072].rearrange("p (j c) -> p j c", j=GS)
        assert 6400 + GS * 3072 <= 2 * WKN
        ublk = [(0, 384), (384, 384), (768, 258)]
        j = 0
        while j < NJ:
            gn = min(GS, NJ - j)
            for jj in range(gn):
                jg = j + jj
                w1t, k1 = wload(w1R[jg].rearrange("p k m -> p (k m)"))
                wv = w1t[:].rearrange("p (k m) -> p k m", k=16)
                for (t0, n) in ublk:
                    b = nb()
                    for kc in range(16):
                        mm(ps[b][:, 0:n], wv[:, kc, :], h2T[:, kc, t0:t0 + n], kc == 0, kc == 15, r=[k1, 'yab'], w=[('ps', b)])
                    cp('act', u32[:, t0:t0 + n], ps[b][:, 0:n], r=[('ps', b)], w=['u32'])
                ts('dve', uc, u32[:, 0:1024], cw[:, jg, 1:2], cw[:, jg, 3:4], ALU.mult, ALU.add, r=['u32', 'cw'], w=['uc'])
                stt(uc[:, 1:1024], u32[:, 0:1023], cw[:, jg, 0:1], uc[:, 1:1024], ALU.mult, ALU.add, r=['u32', 'cw', 'uc'], w=['uc'])
                stt(uc, u32[:, 1:1025], cw[:, jg, 2:3], uc, ALU.mult, ALU.add, r=['u32', 'cw', 'uc'], w=['uc'])
                act(su, uc, AF.Silu, r=['uc'], w=['su'])
                w3t, k3 = wload(w3R[jg].rearrange("p k m -> p (k m)"))
                wv3 = w3t[:].rearrange("p (k m) -> p k m", k=16)
                for bi in range(2):
                    b = nb()
                    for kc in range(16):
                        mm(ps[b][:, :], wv3[:, kc, :], h2T[:, kc, bi * 512:(bi + 1) * 512], kc == 0, kc == 15,
                           r=[k3, 'yab'], w=[('ps', b)])
                    tt('dve', gated[:, jj, bi * 512:(bi + 1) * 512], ps[b][:, :], su[:, bi * 512:(bi + 1) * 512], ALU.mult,
                       r=[('ps', b), 'su'], w=['gated'])
                i = wctr[0] % NST
                wctr[0] += 1
                P.dma(stg[i][:], w2R[jg], w=[('stg', i)])
                tt('dve', w2b[:, jj, :], stg[i][:], gbc[:], ALU.mult, r=[('stg', i), 'gbc'], w=['w2b'])
            if STOP == 'G1':
                P.dead = True
            for t in range(8):
                for n in range(4):
                    b = nb()
                    for jj in range(gn):
                        mm(ps[b][:, :], gated[:, jj, t * 128:(t + 1) * 128], w2b[:, jj, n * 512:(n + 1) * 512],
                           jj == 0, jj == gn - 1, r=['gated', 'w2b'], w=[('ps', b)])
                    tt('dve', xm[:, t, n * 512:(n + 1) * 512], xm[:, t, n * 512:(n + 1) * 512], ps[b][:, :], ALU.add,
                       r=[('ps', b), ('xm', t)], w=[('xm', t)])
            if STOP == 'G2':
                P.dead = True
            j += gn
        P.dead = False
        for t in range(8):
            P.dma(out_d[t * 128:(t + 1) * 128, :], xm[:, t, :], r=[('xm', t)], w=[('out', t)])

        blk = es.enter_context(nc.Block())
        P.emit(blk)
    return nc


GRID_W, WIN_R, WIN_C = 64, 8, 16


def _tables(s):
    rows = 32
    kr = 8

    def glob(rl, cl):
        return (rl, cl) if s == 0 else (31 - rl, 63 - cl)

    def entry(rq_l, a_l):
        idr = np.zeros((128, 64), np.int64)
        idc = np.zeros((128, 64), np.int64)
        val = np.zeros((128, 64), bool)
        for kk in range(2):
            krl = a_l + kk
            for kc in range(64):
                for qc in range(64):
                    if krl < 0 or krl > 31:
                        continue
                    qr, qcg = glob(rq_l, qc)
                    kr_g, kcg = glob(krl, kc)
                    rs_ = min(max(qr - WIN_R // 2, 0), rows - kr)
                    cs_ = min(max(qcg - WIN_C // 2, 0), GRID_W - WIN_C)
                    ok = (rs_ <= kr_g < rs_ + kr) and (cs_ <= kcg < cs_ + WIN_C)
                    if ok:
                        dr = kr_g - qr
                        dc = min(max(kcg - qcg, -(WIN_C - 1)), WIN_C - 1)
                        idr[kk * 64 + kc, qc] = dr + WIN_R - 1
                        idc[kk * 64 + kc, qc] = dc + WIN_C - 1
                        val[kk * 64 + kc, qc] = True
        return idr, idc, val

    ents = []
    for e in range(10):
        dra = e - 5
        rq = 8 if dra % 2 == 0 else 9
        ents.append(entry(rq, rq + dra))
    for e in range(10, 20):
        dra = e - 13
        done = False
        for rq in range(4):
            a = rq + dra
            if a in (0, 2, 4, 6):
                ents.append(entry(rq, a))
                done = True
                break
        if not done:
            ents.append((np.zeros((128, 64), np.int64), np.zeros((128, 64), np.int64), np.zeros((128, 64), bool)))
    return ents


_TAB_CACHE = {}


def _consts(s):
    if s in _TAB_CACHE:
        return _TAB_CACHE[s]
    ents = _tables(s)
    idr = np.stack([e[0] for e in ents])
    idc = np.stack([e[1] for e in ents])
    val = np.stack([e[2] for e in ents])
    maskT = np.where(val, 0.0, NEG).astype(np.float32).transpose(1, 0, 2).copy()
    t = np.arange(1408)
    tg = t if s == 0 else (2047 - t)
    row = (tg // GRID_W).astype(np.float32)
    col = (tg % GRID_W).astype(np.float32)
    nf = 32
    inv = (np.float32(10000.0) ** (-np.arange(nf, dtype=np.float32) / np.float32(nf))).astype(np.float32)
    ang_r = (row[:, None] * inv[None, :]).astype(np.float32)
    ang_c = (col[:, None] * inv[None, :]).astype(np.float32)
    cosr, sinr = np.cos(ang_r), np.sin(ang_r)
    cosc, sinc = np.cos(ang_c), np.sin(ang_c)
    cos = np.concatenate([cosr, cosr, cosc, cosc], axis=1).T
    sin = np.concatenate([sinr, sinr, sinc, sinc], axis=1).T
    rope = np.stack([cos, sin], axis=1).astype(np.float32).copy()
    _TAB_CACHE[s] = (idr, idc, maskT, rope)
    return _TAB_CACHE[s]


def _static_consts():
    cst = np.zeros((128, 384), np.float32)
    cst[:, 0:128] = np.eye(128, dtype=np.float32)
    cst[:, 128:256] = 1.0
    pr = np.zeros((128, 128), np.float32)
    for base in (0, 64):
        for j in range(32):
            pr[base + j + 32, base + j] = -1.0
            pr[base + j, base + j + 32] = 1.0
    cst[:, 256:384] = pr
    m01 = np.ones((128, NT), np.float32)
    m01[:, 0::64] = 0.0
    trim = np.zeros((64, 2, 64), np.uint8)
    s_ = np.arange(64)[:, None]
    t_ = np.arange(64)[None, :]
    trim[:, 0, :] = (t_ >= s_)
    trim[:, 1, :] = (t_ <= s_)
    sel = np.zeros((16, D), np.float32)
    for k in range(16):
        sel[k, k * 128:(k + 1) * 128] = 1.0
    return cst, m01, trim, sel


_NC = None


def kernel(x, c, ctx, c_ctx, ada_w, ada_b, norm1_g, norm2_g, w_in, hgrn_lb_logits, hgrn_norm_g,
           na_q_norm_g, na_k_norm_g, na_rel_bias, w_branch_a, w_branch_b, w_out,
           ffn_w1, ffn_w3, ffn_conv_w, ffn_conv_b, ffn_w2, _only_maps=False):
    global _NC
    f = lambda a: np.ascontiguousarray(np.asarray(a, dtype=np.float32))
    x, c, ctx, c_ctx = f(x), f(c), f(ctx), f(c_ctx)
    ada_w, ada_b, w_in = f(ada_w)[0], f(ada_b)[0], f(w_in)[0]
    n1, n2 = f(norm1_g)[0], f(norm2_g)[0]
    lbl = f(hgrn_lb_logits)
    rel = f(na_rel_bias)[0]
    wa, wb, wo = f(w_branch_a)[0], f(w_branch_b)[0], f(w_out)[0]
    w1, w3, w2 = f(ffn_w1)[0], f(ffn_w3)[0], f(ffn_w2)[0]
    cwt, cbs = f(ffn_conv_w)[0], f(ffn_conv_b)[0]

    def colblk(w, ncols):
        return np.ascontiguousarray(w.reshape(16, 128, ncols // 128, 128).transpose(2, 1, 0, 3))

    adaR = colblk(ada_w, 6 * D)
    adab = np.ascontiguousarray(ada_b.reshape(96, 128).T)
    ng = np.ascontiguousarray(np.stack([n1.reshape(16, 128).T, n2.reshape(16, 128).T], axis=1))
    winB = colblk(w_in, 12288)
    smallg = np.ascontiguousarray(np.stack([f(hgrn_norm_g)[0], f(na_q_norm_g)[0], f(na_k_norm_g)[0]], axis=1))
    wab = np.ascontiguousarray(np.stack([wa, wb]).reshape(2, 8, 128, 16, 128).transpose(0, 3, 2, 1, 4))
    woR = np.ascontiguousarray(wo.reshape(16, 128, 4, 512).transpose(2, 1, 0, 3))
    w1R = colblk(w1, HID)
    w3R = colblk(w3, HID)
    w2R = np.ascontiguousarray(w2.reshape(NJ, 128, D))
    cst, m01, trim, sel = _static_consts()
    winR_s = []
    for s in range(2):
        src = [0, 1, 2, 4, 3, 5, 6, 7] if s == 0 else [0, 2, 1, 4, 3, 5, 6, 7]
        arr = np.empty((96, 128, 16, 128), np.float32)
        for h in range(8):
            for g in range(8):
                arr[h * 8 + g] = winB[src[g] * 8 + h]
        for cch in range(16):
            arr[64 + 2 * cch] = winB[64 + cch]
            arr[64 + 2 * cch + 1] = winB[80 + cch]
        winR_s.append(arr)
    in_maps = []
    for core in range(8):
        b, s = core // 2, core % 2
        idr, idc, maskT, rope = _consts(s)
        xl = x[b] if s == 0 else x[b, ::-1]
        cl = ctx[b] if s == 0 else ctx[b, ::-1]
        ccv = np.stack([c[b].reshape(16, 128).T, c_ctx.reshape(16, 128).T], axis=2)
        ld = lbl if s == 0 else lbl[::-1]
        lblc = ld.reshape(2, 2, 8, 128).transpose(3, 0, 1, 2)
        biasT = rel[:, idr, idc].transpose(2, 0, 1, 3)
        taps = cwt if s == 0 else cwt[::-1]
        convw = np.stack([taps[0], taps[1], taps[2], cbs], axis=1).reshape(NJ, 128, 4).transpose(1, 0, 2)
        in_maps.append({
            "xloc": np.ascontiguousarray(xl), "ctxl": np.ascontiguousarray(cl),
            "cc": np.ascontiguousarray(ccv), "adaR": adaR, "adab": adab, "ng": ng,
            "winR": winR_s[s], "lbl": np.ascontiguousarray(lblc), "smallg": smallg,
            "biasT": np.ascontiguousarray(biasT), "maskT": maskT, "rope": rope,
            "cst": cst, "m01": m01, "trim": trim, "sel": sel, "wab": wab, "woR": woR,
            "w1R": w1R, "w3R": w3R, "w2R": w2R, "convw": np.ascontiguousarray(convw),
        })
    if _only_maps:
        return in_maps
    if _NC is None:
        _NC = build_nc()
    res = run_bass_kernel_spmd(_NC, in_maps, core_ids=list(range(8)))
    out = np.empty((4, SEQ, D), np.float32)
    for core in range(8):
        b, s = core // 2, core % 2
        o = np.asarray(res.results[core]["out"], dtype=np.float32)
        if s == 0:
            out[b, 0:1024] = o
        else:
            out[b, 1024:2048] = o[::-1]
    kernel.last_results = res
    return out
```
